# Optimizing a Trainium2 kernel written in Bass

```python
import jax, jax.numpy as jnp
from jax import lax
import numpy as np

D_MODEL = 1024
BATCH = 8
SEQ = 4096
DEPTH = 1

GRID_W = 64
PLE_DIM = 256
D_FF = 2816
ATTN_HEADS = 8
HEAD_DIM = 64
D_ATTN = ATTN_HEADS * HEAD_DIM
NA_MAX_ROWS = 8
NA_COLS = 16
POOL_GROUPS = 4
POOL_GROUP_DIM = 128
D_POOL = POOL_GROUPS * POOL_GROUP_DIM
POOL_WINDOWS = (2, 4, 8, 16)
RPB_ROWS = 2 * NA_MAX_ROWS - 1
RPB_COLS = 2 * NA_COLS - 1
D_IN = 3 * D_ATTN + D_POOL + 2 * D_MODEL
RMS_EPS = 1e-6

kernel_name = "hybrid_natten_pool_macaron_encoder"


def rms_norm(x, g):
    xf = x.astype(jnp.float32)
    y = xf * lax.rsqrt(jnp.mean(xf * xf, axis=-1, keepdims=True) + RMS_EPS)
    return (y * g.astype(jnp.float32)).astype(x.dtype)


def swiglu(x, w_gate, w_up, w_down):
    return (jax.nn.silu(x @ w_gate) * (x @ w_up)) @ w_down


def neighbourhood_attention(q, k, v, rpb):
    B, S, _ = q.shape
    rows = S // GRID_W
    kr = min(NA_MAX_ROWS, rows)

    def to_grid(t):
        return t.reshape(B, rows, GRID_W, ATTN_HEADS, HEAD_DIM).transpose(0, 3, 1, 2, 4)

    qg = to_grid(q * (HEAD_DIM ** -0.5))
    kg, vg = to_grid(k), to_grid(v)

    cols = jnp.arange(GRID_W)
    col_start = jnp.clip(cols - NA_COLS // 2, 0, GRID_W - NA_COLS)
    col_idx = col_start[:, None] + jnp.arange(NA_COLS)[None, :]
    dc = col_idx - cols[:, None] + (NA_COLS - 1)

    def row_block(r):
        rs = jnp.clip(r - kr // 2, 0, rows - kr)
        k_rows = lax.dynamic_slice_in_dim(kg, rs, kr, axis=2)
        v_rows = lax.dynamic_slice_in_dim(vg, rs, kr, axis=2)
        kw = k_rows[:, :, :, col_idx, :]
        vw = v_rows[:, :, :, col_idx, :]
        q_row = lax.dynamic_index_in_dim(qg, r, axis=2, keepdims=False)
        s = jnp.einsum('bhcd,bhicjd->bhcij', q_row, kw).astype(jnp.float32)
        dr = rs + jnp.arange(kr) - r + (NA_MAX_ROWS - 1)
        bias = rpb[:, dr, :][:, :, dc]
        s = s + bias.transpose(0, 2, 1, 3)[None].astype(jnp.float32)
        pw = jax.nn.softmax(s.reshape(B, ATTN_HEADS, GRID_W, kr * NA_COLS), axis=-1)
        pw = pw.reshape(B, ATTN_HEADS, GRID_W, kr, NA_COLS).astype(v.dtype)
        return jnp.einsum('bhcij,bhicjd->bhcd', pw, vw)

    out = lax.map(row_block, jnp.arange(rows))
    return out.transpose(1, 0, 3, 2, 4).reshape(B, S, D_ATTN)


def multiscale_pool(xp, pool_w, pool_scale):
    B, S, _ = xp.shape
    xg = xp.reshape(B, S, POOL_GROUPS, POOL_GROUP_DIM)
    csum = jnp.cumsum(xg.astype(jnp.float32), axis=1)
    csum = jnp.concatenate([jnp.zeros_like(csum[:, :1]), csum], axis=1)
    half = jnp.array([w // 2 for w in POOL_WINDOWS], dtype=jnp.int32)
    t = jnp.arange(S, dtype=jnp.int32)[:, None]
    lo = jnp.clip(t - half[None, :], 0, S)
    hi = jnp.clip(t + half[None, :], 0, S)
    gidx = jnp.arange(POOL_GROUPS)[None, :]
    window_sum = csum[:, hi, gidx] - csum[:, lo, gidx]
    count = (hi - lo).astype(jnp.float32)[None, :, :, None]
    pooled = (window_sum / count - xg.astype(jnp.float32)).astype(xp.dtype)
    y = jnp.einsum('bsgc,gcd->bsgd', pooled, pool_w).reshape(B, S, D_POOL)
    return y * pool_scale


def hybrid_mixer(u, w_in, rpb, pool_w, pool_scale, w_br_attn, w_br_pool, w_out):
    proj = u @ w_in
    splits = [D_ATTN, 2 * D_ATTN, 3 * D_ATTN, 3 * D_ATTN + D_POOL, 3 * D_ATTN + D_POOL + D_MODEL]
    q, k, v, xp, g_attn, g_pool = jnp.split(proj, splits, axis=-1)
    y_attn = neighbourhood_attention(q, k, v, rpb) @ w_br_attn
    y_pool = multiscale_pool(xp, pool_w, pool_scale) @ w_br_pool
    merged = jax.nn.sigmoid(g_attn) * y_attn + jax.nn.sigmoid(g_pool) * y_pool
    return merged @ w_out


def setup_inputs(seed: int = 0) -> dict:
    key = jax.random.key(seed)
    ks = jax.random.split(key, 32)

    def nrm(k, shape, scale):
        return jax.random.normal(k, shape, jnp.float32) * scale

    def gain(k):
        return 1.0 + 0.05 * jax.random.normal(k, (DEPTH, D_MODEL), jnp.float32)

    return {
        "x": nrm(ks[0], (BATCH, SEQ, D_MODEL), 1.0),
        "p": nrm(ks[1], (DEPTH, BATCH, SEQ, PLE_DIM), 1.0),
        "ffn1_pre_g": gain(ks[2]),
        "ffn1_post_g": gain(ks[3]),
        "ffn1_w_gate": nrm(ks[4], (DEPTH, D_MODEL, D_FF), D_MODEL ** -0.5),
        "ffn1_w_up": nrm(ks[5], (DEPTH, D_MODEL, D_FF), D_MODEL ** -0.5),
        "ffn1_w_down": nrm(ks[6], (DEPTH, D_FF, D_MODEL), D_FF ** -0.5),
        "mix_pre_g": gain(ks[7]),
        "mix_post_g": gain(ks[8]),
        "w_in": nrm(ks[9], (DEPTH, D_MODEL, D_IN), D_MODEL ** -0.5),
        "rpb": nrm(ks[10], (DEPTH, ATTN_HEADS, RPB_ROWS, RPB_COLS), 0.5),
        "pool_w": nrm(ks[11], (DEPTH, POOL_GROUPS, POOL_GROUP_DIM, POOL_GROUP_DIM), POOL_GROUP_DIM ** -0.5),
        "pool_scale": 1.0 + 0.05 * jax.random.normal(ks[12], (DEPTH, D_POOL), jnp.float32),
        "w_br_attn": nrm(ks[13], (DEPTH, D_ATTN, D_MODEL), D_ATTN ** -0.5),
        "w_br_pool": nrm(ks[14], (DEPTH, D_POOL, D_MODEL), D_POOL ** -0.5),
        "w_out": nrm(ks[15], (DEPTH, D_MODEL, D_MODEL), D_MODEL ** -0.5),
        "ffn2_pre_g": gain(ks[16]),
        "ffn2_post_g": gain(ks[17]),
        "ffn2_w_gate": nrm(ks[18], (DEPTH, D_MODEL, D_FF), D_MODEL ** -0.5),
        "ffn2_w_up": nrm(ks[19], (DEPTH, D_MODEL, D_FF), D_MODEL ** -0.5),
        "ffn2_w_down": nrm(ks[20], (DEPTH, D_FF, D_MODEL), D_FF ** -0.5),
        "ple_pre_g": gain(ks[21]),
        "ple_post_g": gain(ks[22]),
        "ple_w_proj": nrm(ks[23], (DEPTH, PLE_DIM, D_MODEL), PLE_DIM ** -0.5),
        "ple_w_gate": nrm(ks[24], (DEPTH, D_MODEL, D_MODEL), D_MODEL ** -0.5),
    }


def reference(x, p, ffn1_pre_g, ffn1_post_g, ffn1_w_gate, ffn1_w_up, ffn1_w_down,
              mix_pre_g, mix_post_g, w_in, rpb, pool_w, pool_scale, w_br_attn, w_br_pool, w_out,
              ffn2_pre_g, ffn2_post_g, ffn2_w_gate, ffn2_w_up, ffn2_w_down,
              ple_pre_g, ple_post_g, ple_w_proj, ple_w_gate):
    h = x
    for i in range(DEPTH):
        f = swiglu(rms_norm(h, ffn1_pre_g[i]), ffn1_w_gate[i], ffn1_w_up[i], ffn1_w_down[i])
        h = h + 0.5 * rms_norm(f, ffn1_post_g[i])
        m = hybrid_mixer(rms_norm(h, mix_pre_g[i]), w_in[i], rpb[i], pool_w[i], pool_scale[i],
                         w_br_attn[i], w_br_pool[i], w_out[i])
        h = h + rms_norm(m, mix_post_g[i])
        f = swiglu(rms_norm(h, ffn2_pre_g[i]), ffn2_w_gate[i], ffn2_w_up[i], ffn2_w_down[i])
        h = h + 0.5 * rms_norm(f, ffn2_post_g[i])
        e = (p[i] @ ple_w_proj[i]) * jax.nn.sigmoid(rms_norm(h, ple_pre_g[i]) @ ple_w_gate[i])
        h = h + rms_norm(e, ple_post_g[i])
    return h
```

```python
import os
import numpy as np
from contextlib import ExitStack
import concourse.bass as bass
import concourse.mybir as mybir
from concourse.bass_utils import run_bass_kernel_spmd

F32 = mybir.dt.float32
BF16 = mybir.dt.bfloat16
AF = mybir.ActivationFunctionType
ALU = mybir.AluOpType

D = 1024
SEQ = 4096
T = 512
NT = SEQ // T
DFF = 2816
NFC = DFF // 128
NEG = -30000.0
NH = 2
NW = 7
FILL = os.environ.get("K_FILL", "1") == "1"
FB = (2, 2, 1)
FA = (2, 1)
WSLOT = 2048
PSN = 6


class Op:
    __slots__ = ("eng", "fn", "deps", "idx", "inc", "semval", "dma_key", "dma_cnt")

    def __init__(self, eng, fn, dma_key=None):
        self.eng = eng
        self.fn = fn
        self.deps = []
        self.idx = -1
        self.inc = False
        self.semval = 0
        self.dma_key = dma_key
        self.dma_cnt = 0


ENGS = ("pe", "act", "dve", "pool", "sp")


class Sched:
    def __init__(self):
        self.ops = {e: [] for e in ENGS}
        self.last_w = {}
        self.readers = {}
        self.dma_count = {}
        self.all_ops = []

    def op(self, eng, fn, reads=(), writes=(), dma_key=None):
        o = Op(eng, fn, dma_key)
        deps = {}

        def add(d):
            if d is None:
                return
            if d.dma_key is None and d.eng == "pe" and eng == "pe" and dma_key is None:
                return
            k = ("dma", d.dma_key) if d.dma_key is not None else ("eng", d.eng)
            cur = deps.get(k)
            if cur is None or (d.dma_key is None and d.idx > cur.idx) or (d.dma_key is not None and d.dma_cnt > cur.dma_cnt):
                deps[k] = d

        for r in reads:
            add(self.last_w.get(r))
        for w in writes:
            add(self.last_w.get(w))
            for rd in self.readers.get(w, ()):
                add(rd)
        o.deps = list(deps.values())
        o.idx = len(self.ops[eng])
        self.ops[eng].append(o)
        self.all_ops.append(o)
        if dma_key is not None:
            self.dma_count[dma_key] = self.dma_count.get(dma_key, 0) + 16
            o.dma_cnt = self.dma_count[dma_key]
        for r in reads:
            lst = self.readers.setdefault(r, [])
            if dma_key is None:
                lst[:] = [x for x in lst if not (x.dma_key is None and x.eng == eng)]
            lst.append(o)
        for w in writes:
            self.last_w[w] = o
            self.readers[w] = []
        return o

    def emit(self, nc, stack):
        for o in self.all_ops:
            for d in o.deps:
                if d.dma_key is None:
                    d.inc = True
        esem = {}
        for e in ENGS:
            esem[e] = stack.enter_context(nc.semaphore("s_" + e))
            c = 0
            for o in self.ops[e]:
                if o.dma_key is None and o.inc:
                    c += 1
                    o.semval = c
        dsem = {}
        for k in self.dma_count:
            dsem[k] = stack.enter_context(nc.semaphore("d%d" % len(dsem)))
        block = stack.enter_context(nc.Block())

        def run(e, h):
            waited = {}
            for o in self.ops[e]:
                for d in o.deps:
                    if d.dma_key is not None:
                        s, v = dsem[d.dma_key], d.dma_cnt
                    else:
                        s, v = esem[d.eng], d.semval
                    if waited.get(s.num, 0) >= v:
                        continue
                    waited[s.num] = v
                    h.wait_ge(s, v)
                if o.fn is None:
                    continue
                ins = o.fn(h)
                if o.dma_key is not None:
                    ins.then_inc(dsem[o.dma_key], 16)
                elif o.inc:
                    ins.then_inc(esem[e], 1)

        @block.tensor
        def _(h):
            run("pe", h)

        @block.scalar
        def _(h):
            run("act", h)

        @block.vector
        def _(h):
            run("dve", h)

        @block.gpsimd
        def _(h):
            run("pool", h)

        @block.sync
        def _(h):
            run("sp", h)


WSHAPES = {
    "ffn1_wg": [D, DFF], "ffn1_wu": [D, DFF], "ffn1_wd": [DFF, D],
    "w_in": [D, 4096], "pool_w": [4, 128, 128], "w_br_attn": [512, D], "w_br_pool": [512, D], "w_out": [D, D],
    "ffn2_wg": [D, DFF], "ffn2_wu": [D, DFF], "ffn2_wd": [DFF, D],
    "ple_wp": [256, D], "ple_wg": [D, D],
}


def build_program(nt_run=NT, stop_stage=99, debug=False):
    nc = bass.Bass("TRN2", target_bir_lowering=False)

    def din(name, shape):
        return nc.dram_tensor(name, shape, F32, kind="ExternalInput").ap()

    xT = din("xT", [D, SEQ])
    pT = din("pT", [256, SEQ])
    Wd = {k: din(k, s) for k, s in WSHAPES.items()}
    gains_d = din("gains", [128, 8, 8])
    pscale_d = din("pscale", [128, 4])
    biasT_d = din("biasT", [5, 128, 5120])
    invc_d = din("invc", [128, 4, 16])
    ident_d = din("ident", [128, 128])
    outT = nc.dram_tensor("outT", [D, SEQ], F32, kind="ExternalOutput").ap()

    dbg = {}
    if debug:
        for nm, shp in (("attn", [128, 4, T]), ("pooled", [128, 4, T]), ("yp2", [128, 4, T]), ("merged", [128, 8, T]), ("q", [128, 4, T]), ("k", [128, 4, T]), ("v", [128, 4, 512])):
            dbg[nm] = nc.dram_tensor("dbg_" + nm, shp, BF16, kind="ExternalOutput").ap()
    xTv = xT.rearrange("(c p) s -> p c s", p=128)
    outTv = outT.rearrange("(c p) s -> p c s", p=128)
    pTv = pT.rearrange("(c p) s -> p c s", p=128)

    def wview(name):
        return Wd[name].rearrange("(k p) n -> p k n", p=128)

    S = Sched()
    with ExitStack() as st:
        def sb(name, shape, dt):
            return st.enter_context(nc.sbuf_tensor(name, shape, dt))

        Hbuf = sb("Hbuf", [128, NH, 8, T], F32)
        bufF = sb("bufF", [128, 8 * T], F32)
        sq = sb("sq", [128, 4, T], BF16)
        xnm = sb("xnm", [128, 2, 8, T], BF16)
        hidA = sb("hidA", [128, NFC * T // 2], F32)
        kT = sb("kT", [128, 3, 4, T], BF16)
        Vr = sb("Vr", [128, 3, 4, 512], BF16)
        qT = sb("qT", [128, 2, 4, T], BF16)
        xp = sb("xp", [128, 3, 4, T + 16], BF16)
        PT = sb("PT", [128, 2, 1280], BF16)
        attnT = sb("attnT", [128, 2, 4, T], BF16)
        rden = sb("rden", [128, 2, 128], F32)
        biasI = sb("biasI", [128, 5120], BF16)
        sig = sb("sig", [128, 2, 2, T], F32)
        rstd = sb("rstd", [128, T], F32)
        tmp = sb("tmp", [128, 2, T], F32)
        pTb = sb("pTb", [128, 2, T], BF16)
        wring = [sb("w%d" % i, [128, WSLOT], BF16) for i in range(NW)]
        ident = sb("identb", [128, 128], BF16)
        ones = sb("onesb", [128, 128], BF16)
        gs = sb("gs", [128, 8, 8], F32)
        pscale = sb("pscale_sb", [128, 4], F32)
        epst = sb("epst", [128, 1], F32)
        invc = sb("invc_sb", [128, 4, 16], F32)
        etmp = sb("etmp", [128, 8], F32)
        ps = st.enter_context(nc.psum_tensor("ps", [128, 8 * 512], F32))

        def bank(b):
            return ps[:, b * 512:(b + 1) * 512]

        def pk(b):
            return [("ps", b)]

        fF = bufF[:].rearrange("p (c t) -> p c t", c=8)
        bF16 = bufF[:].bitcast(BF16)
        xn = bF16[:, 0:8 * T].rearrange("p (c t) -> p c t", c=8)
        pooled = bF16[:, 8 * T:12 * T].rearrange("p (c t) -> p c t", c=4)
        yp2 = bF16[:, 12 * T:16 * T].rearrange("p (c t) -> p c t", c=4)
        hid = hidA[:].bitcast(BF16).rearrange("p (c t) -> p c t", c=NFC)
        fH = hidA[:, 0:8 * T].rearrange("p (c t) -> p c t", c=8)
        ptA = hidA[:, 8 * T:8 * T + 544]
        ptB = hidA[:, 8 * T + 768:8 * T + 768 + 544]

        def kF(c):
            return ("F", c)

        def kxn(c):
            return ("F", c // 2)

        def kfH(c):
            return [("hid", 2 * c), ("hid", 2 * c + 1)]

        KPTA = [("hid", 16), ("hid", 17), ("hid", 18)]
        KPTB = [("hid", 19), ("hid", 20), ("hid", 21)]

        S.op("sp", lambda h: h.dma_start(out=gs[:], in_=gains_d), writes=["gs"], dma_key="c_gs")
        S.op("sp", lambda h: h.dma_start(out=pscale[:], in_=pscale_d), writes=["pscale"], dma_key="c_ps")
        S.op("sp", lambda h: h.dma_start(out=invc[:], in_=invc_d), writes=["invc"], dma_key="c_ic")
        S.op("pool", lambda h: h.dma_start(out=ident[:], in_=ident_d), writes=["ident"], dma_key="c_id")
        S.op("pool", lambda h: h.dma_start(out=biasI[:], in_=biasT_d[0]), writes=["biasI"], dma_key="c_bi")
        S.op("dve", lambda h: h.memset(ones[:], 1.0), writes=["ones"])
        S.op("dve", lambda h: h.memset(epst[:], 1e-6), writes=["epst"])
        S.op("dve", lambda h: h.memset(xp[:, 0, :, 0:8], 0.0), writes=[("xpl", 0)])
        for v in (1, 5):
            S.op("dve", (lambda h, v=v: h.tensor_scalar(out=gs[:, v, :], in0=gs[:, v, :], scalar1=0.5, scalar2=None, op0=ALU.mult)),
                 reads=["gs"], writes=["gs"])

        wctr = [0]

        def wload(src, kc, ncols):
            i = wctr[0] % NW
            wctr[0] += 1
            dst = wring[i][:, 0:kc * ncols].rearrange("p (k n) -> p k n", k=kc)
            S.op("pool", lambda h: h.dma_start(out=dst, in_=src), writes=[("w", i)], dma_key=("w", i))
            return ("w", i), dst

        sqctr = [0]

        def norm_stats(src, keys):
            for c in range(8):
                j = sqctr[0] % 4
                sqctr[0] += 1
                S.op("act", (lambda h, c=c, j=j: h.activation(out=sq[:, j, :], in_=src(c), func=AF.Square)),
                     reads=keys(c), writes=[("sq", j)])
                S.op("pe", (lambda h, c=c, j=j: h.matmul(bank(PSN), lhsT=ones[:], rhs=sq[:, j, :], start=(c == 0), stop=(c == 7))),
                     reads=[("sq", j), "ones"], writes=[("ps", PSN)])
            S.op("act", lambda h: h.activation(out=rstd[:], in_=bank(PSN), func=AF.Sqrt, bias=epst[:, 0:1], scale=1.0 / D),
                 reads=[("ps", PSN), "epst"], writes=["rstd"])
            S.op("dve", lambda h: h.reciprocal(out=rstd[:], in_=rstd[:]), reads=["rstd"], writes=["rstd"])

        def hkeys(t):
            return lambda c: [("h", t % NH, c)]

        def hsrc(t):
            return lambda c: Hbuf[:, t % NH, c, :]

        def pre_norm(t, v, dst, dkeys):
            norm_stats(hsrc(t), hkeys(t))
            for c in range(8):
                S.op("dve", (lambda h, c=c: h.scalar_tensor_tensor(out=dst(c), in0=Hbuf[:, t % NH, c, :], scalar=gs[:, v, c:c + 1],
                                                                     in1=rstd[:], op0=ALU.mult, op1=ALU.mult)),
                     reads=[("h", t % NH, c), "rstd", "gs"], writes=dkeys(c))

        def post_norm(t, v, src, keys):
            norm_stats(src, keys)
            for c in range(8):
                i = c % 2
                S.op("dve", (lambda h, c=c, i=i: h.tensor_tensor(out=tmp[:, i, :], in0=src(c), in1=rstd[:], op=ALU.mult)),
                     reads=keys(c) + ["rstd"], writes=[("tmp", i)])
                S.op("dve", (lambda h, c=c, i=i: h.scalar_tensor_tensor(out=Hbuf[:, t % NH, c, :], in0=tmp[:, i, :], scalar=gs[:, v, c:c + 1],
                                                                          in1=Hbuf[:, t % NH, c, :], op0=ALU.mult, op1=ALU.add)),
                     reads=[("tmp", i), "gs", ("h", t % NH, c)], writes=[("h", t % NH, c)])

        obank = [0]

        def next_obank():
            b = (4, 5, 7)[obank[0] % 3]
            obank[0] += 1
            return b

        def proj_fm(wname, col0, nchunks, rhs, rkeys, evac):
            wv = wview(wname)
            kc = wv.shape[1]
            for half in range(nchunks // 2):
                wk, wt = wload(wv[:, :, col0 + half * 256:col0 + (half + 1) * 256], kc, 256)
                for j in range(2):
                    ci = half * 2 + j
                    b = next_obank()
                    for k in range(kc):
                        S.op("pe", (lambda h, b=b, k=k, j=j, wt=wt: h.matmul(bank(b), lhsT=wt[:, k, j * 128:(j + 1) * 128], rhs=rhs(k),
                                                                              start=(k == 0), stop=(k == kc - 1))),
                             reads=[wk] + rkeys(k), writes=[("ps", b)])
                    evac(ci, b)

        def ffn(t, vpre, vpost, pfx, hook=None):
            pre_norm(t, vpre, lambda c: xn[:, c, :], lambda c: [kxn(c)])
            wg, wu, wd = wview(pfx + "_wg"), wview(pfx + "_wu"), wview(pfx + "_wd")
            for grp in range(NFC // 2):
                kg, tg = wload(wg[:, :, grp * 256:(grp + 1) * 256], 8, 256)
                ku, tu = wload(wu[:, :, grp * 256:(grp + 1) * 256], 8, 256)
                for j in range(2):
                    fc = 2 * grp + j
                    pb = (fc % 2) * 2
                    for (kk, tt, b) in ((kg, tg, pb), (ku, tu, pb + 1)):
                        for k in range(8):
                            S.op("pe", (lambda h, b=b, k=k, j=j, tt=tt: h.matmul(bank(b), lhsT=tt[:, k, j * 128:(j + 1) * 128], rhs=xn[:, k, :],
                                                                                  start=(k == 0), stop=(k == 7))),
                                 reads=[kk, kxn(k)], writes=[("ps", b)])
                    s = fc % 2
                    S.op("act", (lambda h, s=s, pb=pb: h.activation(out=sig[:, s, 0, :], in_=bank(pb), func=AF.Silu)),
                         reads=[("ps", pb)], writes=[("sig", s, 0)])
                    S.op("dve", (lambda h, s=s, pb=pb, fc=fc: h.tensor_tensor(out=hid[:, fc, :], in0=bank(pb + 1), in1=sig[:, s, 0, :], op=ALU.mult)),
                         reads=[("ps", pb + 1), ("sig", s, 0)], writes=[("hid", fc)])
            pieces = ((0, 8), (8, 16), (16, 22))
            for db in range(4):
                wl = [wload(wd[:, f0:f1, db * 256:(db + 1) * 256], f1 - f0, 256) for (f0, f1) in pieces]
                for j in range(2):
                    dc = 2 * db + j
                    b = 4 + dc % 2
                    for fc in range(NFC):
                        pi = 0 if fc < 8 else (1 if fc < 16 else 2)
                        wk, wt = wl[pi]
                        f0 = pieces[pi][0]
                        S.op("pe", (lambda h, b=b, fc=fc, f0=f0, j=j, wt=wt: h.matmul(bank(b), lhsT=wt[:, fc - f0, j * 128:(j + 1) * 128], rhs=hid[:, fc, :],
                                                                                     start=(fc == 0), stop=(fc == NFC - 1))),
                             reads=[wk, ("hid", fc)], writes=[("ps", b)])
                    S.op("act", (lambda h, b=b, dc=dc: h.activation(out=fF[:, dc, :], in_=bank(b), func=AF.Copy)),
                         reads=[("ps", b)], writes=[kF(dc)])
            if hook is not None:
                hook()
            post_norm(t, vpost, lambda c: fF[:, c, :], lambda c: [kF(c)])

        def load_x(t):
            s = t % NH
            S.op("sp", lambda h: h.dma_start(out=Hbuf[:, s, :, :], in_=xTv[:, :, t * T:(t + 1) * T]),
                 writes=[("h", s, c) for c in range(8)], dma_key=("x", s))

        out_ops = []

        def store_h(t):
            s = t % NH
            S.op("sp", lambda h: h.dma_start(out=outTv[:, :, t * T:(t + 1) * T], in_=Hbuf[:, s, :, :]),
                 reads=[("h", s, c) for c in range(8)], writes=[("outd", s)], dma_key=("o", s))

        def stage_A(t):
            load_x(t)
            fill(t - 1, FA[0])
            ffn(t, 0, 1, "ffn1", hook=lambda: fill(t - 1, FA[1]))
            if stop_stage <= 1:
                return
            m = t % 2
            s3 = t % 3
            pre_norm(t, 2, lambda c: xnm[:, m, c, :], lambda c: [("xm", m, c)])
            rhs = lambda k: xnm[:, m, k, :]
            rk = lambda k: [("xm", m, k)]

            def evac_k(ci, b):
                S.op("act", lambda h: h.activation(out=kT[:, s3, ci, :], in_=bank(b), func=AF.Copy),
                     reads=[("ps", b)], writes=[("kT", s3, ci)])

            proj_fm("w_in", 512, 4, rhs, rk, evac_k)

            def evac_xp(ci, b):
                S.op("dve", lambda h: h.tensor_copy(out=xp[:, s3, ci, 8:8 + T], in_=bank(b)),
                     reads=[("ps", b)], writes=[("xpc", s3, ci)])
                S.op("dve", lambda h: h.tensor_copy(out=xp[:, (s3 + 2) % 3, ci, 8 + T:16 + T], in_=bank(b)[:, 0:8]),
                     reads=[("ps", b)], writes=[("xpr", (s3 + 2) % 3, ci)])
                S.op("dve", lambda h: h.tensor_copy(out=xp[:, (s3 + 1) % 3, ci, 0:8], in_=bank(b)[:, T - 8:T]),
                     reads=[("ps", b)], writes=[("xpl", (s3 + 1) % 3, ci)])
                if t == NT - 1:
                    S.op("dve", lambda h: h.memset(xp[:, s3, ci, 8 + T:16 + T], 0.0), writes=[("xpr", s3, ci)])

            proj_fm("w_in", 1536, 4, rhs, rk, evac_xp)
            wv = wview("w_in")
            for piece in range(2):
                wk, wt = wload(wv[:, :, 1024 + piece * 256:1024 + (piece + 1) * 256], 8, 256)
                for blk in range(4):
                    b = next_obank()
                    for k in range(8):
                        S.op("pe", (lambda h, b=b, k=k, blk=blk, wt=wt: h.matmul(bank(b)[:, 0:256], lhsT=xnm[:, m, k, blk * 128:(blk + 1) * 128], rhs=wt[:, k, :],
                                                                                  start=(k == 0), stop=(k == 7))),
                             reads=[wk, ("xm", m, k)], writes=[("ps", b)])
                    S.op("dve", (lambda h, b=b, blk=blk, piece=piece: h.tensor_copy(out=Vr[:, s3, blk, piece * 256:(piece + 1) * 256], in_=bank(b)[:, 0:256])),
                         reads=[("ps", b)], writes=[("V", s3, blk, piece)])

        def xpkeys(s3, g):
            return [("xpc", s3, g), ("xpr", s3, g), ("xpl", s3, g), ("xpl", s3)]

        def emit_qproj(t):
            m = t % 2
            qdone[t] = True

            def evac_q(ci, b):
                S.op("act", lambda h: h.mul(out=qT[:, m, ci, :], in_=bank(b), mul=0.125), reads=[("ps", b)], writes=[("q", m, ci)])

            proj_fm("w_in", 0, 4, lambda k: xnm[:, m, k, :], lambda k: [("xm", m, k)], evac_q)

        ITEMS = [(rp, c) for rp in range(4) for c in range(4)]
        anext = [0] * (NT + 1)
        qdone = [False] * (NT + 1)
        apar = [0]
        apend = [None]

        def geom(t, rp):
            r = 8 * t + 2 * rp
            kb = min(max(r - 4, 0), 54)
            var = {0: 1, 2: 2, 60: 3, 62: 4}.get(r, 0)
            return r, kb, var

        def emit_S(t, i, sbi):
            rp, c = ITEMS[i]
            r, kb, var = geom(t, rp)
            m = t % 2
            if var == 0:
                bk = ["biasI"]
                bview = lambda hh, j: biasI[:, ((2 * c + hh) * 5 + j) * 128:((2 * c + hh) * 5 + j + 1) * 128]
            else:
                wk, wt = wload(biasT_d[var][:, c * 1280:(c + 1) * 1280].rearrange("p (k n) -> p k n", k=1), 1, 1280)
                bk = [wk]
                bview = lambda hh, j, wt=wt: wt[:, 0, (hh * 5 + j) * 128:(hh * 5 + j + 1) * 128]
            for hh in range(2):
                for j in range(5):
                    krow = kb + 2 * j
                    ks = (krow // 8) % 3
                    koff = (krow % 8) * 64
                    if j < 4:
                        bnk = sbi * 3 + hh
                        out = bank(bnk)[:, j * 128:(j + 1) * 128]
                    else:
                        bnk = sbi * 3 + 2
                        out = bank(bnk)[:, hh * 128:(hh + 1) * 128]
                    S.op("pe", (lambda h, out=out, hh=hh, ks=ks, koff=koff: h.matmul(out, lhsT=kT[64 * hh:64 * hh + 64, ks, c, koff:koff + 128],
                                                                                   rhs=qT[64 * hh:64 * hh + 64, m, c, rp * 128:(rp + 1) * 128], start=True, stop=False)),
                         reads=[("kT", ks, c), ("q", m, c)], writes=[("ps", bnk)])
                    S.op("pe", (lambda h, out=out, hh=hh, j=j: h.matmul(out, lhsT=ident[:], rhs=bview(hh, j), start=False, stop=True)),
                         reads=["ident"] + bk, writes=[("ps", bnk)])
            for (bnk, c0, n) in ((sbi * 3, 0, 512), (sbi * 3 + 1, 512, 512), (sbi * 3 + 2, 1024, 256)):
                S.op("act", (lambda h, bnk=bnk, c0=c0, n=n: h.activation(out=PT[:, sbi, c0:c0 + n], in_=bank(bnk)[:, 0:n], func=AF.Exp)),
                     reads=[("ps", bnk)], writes=[("PT", sbi, c0)])

        def emit_PV(t, i, sbi):
            rp, c = ITEMS[i]
            r, kb, var = geom(t, rp)
            m = t % 2
            nd = bank(3 * sbi + 2)[:, 256:512]
            ndk = ("ps", 3 * sbi + 2)
            for hh in range(2):
                hd = 2 * c + hh
                for which in range(2):
                    for j in range(5):
                        krow = kb + 2 * j
                        ks = (krow // 8) % 3
                        blk = (krow % 8) // 2
                        c0 = (hh * 512 + j * 128) if j < 4 else (1024 + hh * 128)
                        if which == 0:
                            S.op("pe", (lambda h, hh=hh, hd=hd, ks=ks, blk=blk, c0=c0, j=j: h.matmul(nd[64 * hh:64 * hh + 64, 0:128], lhsT=Vr[:, ks, blk, hd * 64:(hd + 1) * 64],
                                                                                                   rhs=PT[:, sbi, c0:c0 + 128], start=(j == 0), stop=(j == 4))),
                                 reads=[("V", ks, blk, hd // 4), ("PT", sbi, (c0 // 512) * 512)], writes=[ndk])
                        else:
                            S.op("pe", (lambda h, hh=hh, c0=c0, j=j: h.matmul(nd[64 * hh:64 * hh + 64, 128:256], lhsT=ones[:, 0:64],
                                                                               rhs=PT[:, sbi, c0:c0 + 128], start=(j == 0), stop=(j == 4))),
                                 reads=["ones", ("PT", sbi, (c0 // 512) * 512)], writes=[ndk])
            S.op("dve", lambda h: h.reciprocal(out=rden[:, sbi, :], in_=nd[:, 128:256]), reads=[ndk], writes=[("rden", sbi)])
            S.op("dve", lambda h: h.tensor_tensor(out=attnT[:, m, c, rp * 128:(rp + 1) * 128], in0=nd[:, 0:128], in1=rden[:, sbi, :], op=ALU.mult),
                 reads=[ndk, ("rden", sbi)], writes=[("attn", m, c)])

        def attn_run(t, n, limit=16, flush=True):
            if t < 0 or t >= NT or not qdone[t]:
                return
            for _ in range(n):
                if anext[t] >= limit:
                    break
                i = anext[t]
                anext[t] += 1
                sbi = apar[0] % 2
                apar[0] += 1
                emit_S(t, i, sbi)
                if apend[0] is not None:
                    emit_PV(*apend[0])
                apend[0] = (t, i, sbi)
            if flush and apend[0] is not None:
                emit_PV(*apend[0])
                apend[0] = None

        def fill(t, n):
            if FILL:
                attn_run(t, n, limit=8)

        def stage_B(t):
            m = t % 2
            s3 = t % 3
            rhs = lambda k: xnm[:, m, k, :]
            rk = lambda k: [("xm", m, k)]
            if t == 0:
                emit_qproj(0)
            if t + 1 < NT:
                emit_qproj(t + 1)
            attn_run(t, 16)

            for g in range(4):
                w = (2, 4, 8, 16)[g]
                L = T + 16
                X = xp[:, s3, g, :]
                S.op("dve", (lambda h, X=X, L=L: h.tensor_tensor(out=ptA[:, 1:L], in0=X[:, 0:L - 1], in1=X[:, 1:L], op=ALU.add)),
                     reads=xpkeys(s3, g), writes=KPTA)
                cur, curk, oth, othk = ptA, KPTA, ptB, KPTB
                lo, hi = 1, L
                sh = 1
                while sh * 2 < w:
                    nlo, nhi = lo + sh, hi - sh
                    S.op("dve", (lambda h, cur=cur, oth=oth, nlo=nlo, nhi=nhi, sh=sh: h.tensor_tensor(out=oth[:, nlo:nhi], in0=cur[:, nlo - sh:nhi - sh],
                                                                                                       in1=cur[:, nlo + sh:nhi + sh], op=ALU.add)),
                         reads=curk, writes=othk)
                    cur, curk, oth, othk = oth, othk, cur, curk
                    lo, hi = nlo, nhi
                    sh *= 2
                assert lo <= 8 and hi >= 8 + T
                S.op("dve", (lambda h, cur=cur, g=g, w=w, X=X: h.scalar_tensor_tensor(out=pooled[:, g, :], in0=cur[:, 8:8 + T], scalar=1.0 / w, in1=X[:, 8:8 + T],
                                                                                 op0=ALU.mult, op1=ALU.subtract)),
                     reads=curk + xpkeys(s3, g), writes=[("F", 4 + g // 2)])
                for (cond, off, io) in ((t == 0, 0, 0), (t == NT - 1, T - 8, 8)):
                    if cond:
                        S.op("dve", (lambda h, cur=cur, g=g, off=off, io=io: h.tensor_tensor(out=etmp[:], in0=cur[:, 8 + off:16 + off], in1=invc[:, g, io:io + 8], op=ALU.mult)),
                             reads=curk + ["invc"], writes=["etmp"])
                        S.op("dve", (lambda h, g=g, off=off, X=X: h.tensor_tensor(out=pooled[:, g, off:off + 8], in0=etmp[:], in1=X[:, 8 + off:16 + off], op=ALU.subtract)),
                             reads=["etmp"] + xpkeys(s3, g), writes=[("F", 4 + g // 2)])
            pwk, pwt = wload(Wd["pool_w"].rearrange("g c d -> c g d"), 4, 128)
            for g in range(4):
                b = next_obank()
                S.op("pe", (lambda h, b=b, g=g: h.matmul(bank(b), lhsT=pwt[:, g, :], rhs=pooled[:, g, :], start=True, stop=True)),
                     reads=[pwk, ("F", 4 + g // 2)], writes=[("ps", b)])
                S.op("dve", (lambda h, b=b, g=g: h.tensor_scalar(out=yp2[:, g, :], in0=bank(b), scalar1=pscale[:, g:g + 1], scalar2=None, op0=ALU.mult)),
                     reads=[("ps", b), "pscale"], writes=[("F", 6 + g // 2)])

            wba, wbp, win = wview("w_br_attn"), wview("w_br_pool"), wview("w_in")
            for db in range(4):
                ka, ta = wload(wba[:, :, db * 256:(db + 1) * 256], 4, 256)
                kp, tp = wload(wbp[:, :, db * 256:(db + 1) * 256], 4, 256)
                kga, tga = wload(win[:, :, 2048 + db * 256:2048 + (db + 1) * 256], 8, 256)
                kgp, tgp = wload(win[:, :, 3072 + db * 256:3072 + (db + 1) * 256], 8, 256)
                for j in range(2):
                    dc = 2 * db + j
                    s = dc % 2
                    b0 = s * 4
                    for k in range(4):
                        S.op("pe", (lambda h, k=k, j=j, b0=b0, ta=ta: h.matmul(bank(b0), lhsT=ta[:, k, j * 128:(j + 1) * 128], rhs=attnT[:, m, k, :], start=(k == 0), stop=(k == 3))),
                             reads=[ka, ("attn", m, k)], writes=pk(b0))
                    for k in range(4):
                        S.op("pe", (lambda h, k=k, j=j, b0=b0, tp=tp: h.matmul(bank(b0 + 1), lhsT=tp[:, k, j * 128:(j + 1) * 128], rhs=yp2[:, k, :], start=(k == 0), stop=(k == 3))),
                             reads=[kp, ("F", 6 + k // 2)], writes=pk(b0 + 1))
                    for (kk, tt, bo) in ((kga, tga, 2), (kgp, tgp, 3)):
                        for k in range(8):
                            S.op("pe", (lambda h, k=k, j=j, b0=b0, tt=tt, bo=bo: h.matmul(bank(b0 + bo), lhsT=tt[:, k, j * 128:(j + 1) * 128], rhs=xnm[:, m, k, :],
                                                                                       start=(k == 0), stop=(k == 7))),
                                 reads=[kk, ("xm", m, k)], writes=pk(b0 + bo))
                    for q in range(2):
                        S.op("act", (lambda h, q=q, s=s, b0=b0: h.activation(out=sig[:, s, q, :], in_=bank(b0 + 2 + q), func=AF.Sigmoid)),
                             reads=pk(b0 + 2 + q), writes=[("sig", s, q)])
                        S.op("dve", (lambda h, q=q, s=s, b0=b0: h.tensor_tensor(out=sig[:, s, q, :], in0=bank(b0 + q), in1=sig[:, s, q, :], op=ALU.mult)),
                             reads=pk(b0 + q) + [("sig", s, q)], writes=[("sig", s, q)])
                    S.op("dve", (lambda h, s=s, dc=dc: h.tensor_tensor(out=xn[:, dc, :], in0=sig[:, s, 0, :], in1=sig[:, s, 1, :], op=ALU.add)),
                         reads=[("sig", s, 0), ("sig", s, 1)], writes=[kxn(dc)])

            def evac_m(ci, b):
                S.op("act", lambda h: h.activation(out=fH[:, ci, :], in_=bank(b), func=AF.Copy), reads=[("ps", b)], writes=kfH(ci))

            if debug and t == 0:
                def dump(nm, src, keys):
                    S.op("sp", lambda h: h.dma_start(out=dbg[nm], in_=src), reads=keys, writes=[("dbg", nm)], dma_key=("dbg", nm))
                dump("attn", attnT[:, 0], [("attn", 0, c) for c in range(4)])
                dump("pooled", pooled, [("F", 4), ("F", 5)])
                dump("yp2", yp2, [("F", 6), ("F", 7)])
                dump("merged", xn, [("F", c) for c in range(4)])
                dump("q", qT[:, 0], [("q", 0, c) for c in range(4)])
                dump("k", kT[:, 0, :, :], [("kT", 0, c) for c in range(4)])
                dump("v", Vr[:, 0, :, :], [("V", 0, b, p_) for b in range(4) for p_ in range(2)])
            proj_fm("w_out", 0, 8, lambda k: xn[:, k, :], lambda k: [kxn(k)], evac_m)
            fill(t + 1, FB[0])
            post_norm(t, 3, lambda c: fH[:, c, :], kfH)
            if stop_stage <= 2:
                store_h(t)
                return
            ffn(t, 4, 5, "ffn2", hook=lambda: fill(t + 1, FB[1]))
            if stop_stage <= 3:
                store_h(t)
                return
            pre_norm(t, 6, lambda c: xn[:, c, :], lambda c: [kxn(c)])
            S.op("pool", lambda h: h.dma_start(out=pTb[:], in_=pTv[:, :, t * T:(t + 1) * T]), writes=["pTb"], dma_key="pTb")
            wp, wgt = wview("ple_wp"), wview("ple_wg")
            for db in range(4):
                kp_, tp_ = wload(wp[:, :, db * 256:(db + 1) * 256], 2, 256)
                kg_, tg_ = wload(wgt[:, :, db * 256:(db + 1) * 256], 8, 256)
                for j in range(2):
                    dc = 2 * db + j
                    s = dc % 2
                    b0 = s * 2
                    for k in range(2):
                        S.op("pe", (lambda h, k=k, j=j, b0=b0, tp_=tp_: h.matmul(bank(b0), lhsT=tp_[:, k, j * 128:(j + 1) * 128], rhs=pTb[:, k, :], start=(k == 0), stop=(k == 1))),
                             reads=[kp_, "pTb"], writes=[("ps", b0)])
                    for k in range(8):
                        S.op("pe", (lambda h, k=k, j=j, b0=b0, tg_=tg_: h.matmul(bank(b0 + 1), lhsT=tg_[:, k, j * 128:(j + 1) * 128], rhs=xn[:, k, :], start=(k == 0), stop=(k == 7))),
                             reads=[kg_, kxn(k)], writes=[("ps", b0 + 1)])
                    S.op("act", (lambda h, s=s, b0=b0: h.activation(out=sig[:, s, 0, :], in_=bank(b0 + 1), func=AF.Sigmoid)),
                         reads=[("ps", b0 + 1)], writes=[("sig", s, 0)])
                    S.op("dve", (lambda h, s=s, b0=b0, dc=dc: h.tensor_tensor(out=fH[:, dc, :], in0=bank(b0), in1=sig[:, s, 0, :], op=ALU.mult)),
                         reads=[("ps", b0), ("sig", s, 0)], writes=kfH(dc))
            fill(t + 1, FB[2])
            post_norm(t, 7, lambda c: fH[:, c, :], kfH)
            store_h(t)

        if stop_stage <= 1:
            for t in range(nt_run):
                stage_A(t)
                store_h(t)
        else:
            stage_A(0)
            for t in range(nt_run):
                if t + 1 < NT:
                    stage_A(t + 1)
                stage_B(t)
        S.op("sp", None, reads=[("outd", s) for s in range(NH)] + [("dbg", nm) for nm in dbg])
        S.emit(nc, st)
    return nc


def _bias_table(rpb):
    out = np.full((5, 128, 8, 5, 128), NEG, dtype=np.float32)
    kl, kc = np.divmod(np.arange(128), 64)
    ql, qc = np.divmod(np.arange(128), 64)
    cs = np.clip(qc - 8, 0, 48)
    for var, r in enumerate((4, 0, 2, 60, 62)):
        kb = min(max(r - 4, 0), 54)
        for j in range(5):
            key_row = kb + 2 * j + kl
            q_row = r + ql
            rs = np.clip(q_row - 4, 0, 56)
            vr = (key_row[:, None] >= rs[None, :]) & (key_row[:, None] < rs[None, :] + 8)
            vc = (kc[:, None] >= cs[None, :]) & (kc[:, None] < cs[None, :] + 16)
            dr = np.clip(key_row[:, None] - q_row[None, :] + 7, 0, 14)
            dcc = np.clip(kc[:, None] - qc[None, :] + 15, 0, 30)
            valid = vr & vc
            for h in range(8):
                out[var, :, h, j, :] = np.where(valid, rpb[h][dr, dcc], np.float32(NEG))
    return out.reshape(5, 128, 5120)


def _chunked(v):
    return np.ascontiguousarray(v.reshape(-1, 128).T)


def prepare_inputs(inputs):
    g = lambda k: np.asarray(inputs[k], dtype=np.float32)
    shared = {
        "ffn1_wg": g("ffn1_w_gate")[0], "ffn1_wu": g("ffn1_w_up")[0], "ffn1_wd": g("ffn1_w_down")[0],
        "w_in": g("w_in")[0], "pool_w": g("pool_w")[0], "w_br_attn": g("w_br_attn")[0], "w_br_pool": g("w_br_pool")[0],
        "w_out": g("w_out")[0],
        "ffn2_wg": g("ffn2_w_gate")[0], "ffn2_wu": g("ffn2_w_up")[0], "ffn2_wd": g("ffn2_w_down")[0],
        "ple_wp": g("ple_w_proj")[0], "ple_wg": g("ple_w_gate")[0],
    }
    names = ["ffn1_pre_g", "ffn1_post_g", "mix_pre_g", "mix_post_g", "ffn2_pre_g", "ffn2_post_g", "ple_pre_g", "ple_post_g"]
    shared["gains"] = np.ascontiguousarray(np.stack([_chunked(g(n)[0]) for n in names], axis=1))
    shared["pscale"] = _chunked(g("pool_scale")[0])
    shared["biasT"] = _bias_table(g("rpb")[0])
    invc = np.zeros((128, 4, 16), np.float32)
    for gi, w in enumerate((2, 4, 8, 16)):
        half = w // 2
        for i in range(8):
            tl = i
            tr = SEQ - 8 + i
            invc[:, gi, i] = 1.0 / (min(tl + half, SEQ) - max(tl - half, 0))
            invc[:, gi, 8 + i] = 1.0 / (min(tr + half, SEQ) - max(tr - half, 0))
    shared["invc"] = invc
    shared["ident"] = np.eye(128, dtype=np.float32)
    shared = {k: np.ascontiguousarray(v) for k, v in shared.items()}
    x = g("x")
    p = g("p")[0]
    maps = []
    for b in range(x.shape[0]):
        m = dict(shared)
        m["xT"] = np.ascontiguousarray(x[b].T)
        m["pT"] = np.ascontiguousarray(p[b].T)
        maps.append(m)
    return maps


_NC_CACHE = {}


def kernel(**inputs):
    maps = prepare_inputs(inputs)
    if "nc" not in _NC_CACHE:
        _NC_CACHE["nc"] = build_program()
    nc = _NC_CACHE["nc"]
    res = run_bass_kernel_spmd(nc, maps, core_ids=list(range(8)))
    out = np.stack([np.ascontiguousarray(r["outT"].T) for r in res.results], axis=0)
    return out.astype(np.float32)
```

```python
import os
import numpy as np
from contextlib import ExitStack
import concourse.bass as bass
import concourse.mybir as mybir
from concourse.bass_utils import run_bass_kernel_spmd

F32 = mybir.dt.float32
BF16 = mybir.dt.bfloat16
AF = mybir.ActivationFunctionType
ALU = mybir.AluOpType

D = 1024
SEQ = 4096
T = 512
NT = SEQ // T
DFF = 2816
NFC = DFF // 128
NEG = -30000.0
NH = 2
NW = 7
FILL = os.environ.get("K_FILL", "1") == "1"
FB = tuple(int(v) for v in os.environ.get("K_FB", "0,6,0,6,0,6").split(","))
FA = tuple(int(v) for v in os.environ.get("K_FA", "0,0,6").split(","))
WSLOT = 2048
PSN = 6


class Op:
    __slots__ = ("eng", "fn", "deps", "idx", "inc", "semval", "dma_key", "dma_cnt")

    def __init__(self, eng, fn, dma_key=None):
        self.eng = eng
        self.fn = fn
        self.deps = []
        self.idx = -1
        self.inc = False
        self.semval = 0
        self.dma_key = dma_key
        self.dma_cnt = 0


ENGS = ("pe", "act", "dve", "pool", "sp")


class Sched:
    def __init__(self):
        self.ops = {e: [] for e in ENGS}
        self.last_w = {}
        self.readers = {}
        self.dma_count = {}
        self.all_ops = []

    def op(self, eng, fn, reads=(), writes=(), dma_key=None):
        o = Op(eng, fn, dma_key)
        deps = {}

        def add(d):
            if d is None:
                return
            if d.dma_key is None and d.eng == "pe" and eng == "pe" and dma_key is None:
                return
            k = ("dma", d.dma_key) if d.dma_key is not None else ("eng", d.eng)
            cur = deps.get(k)
            if cur is None or (d.dma_key is None and d.idx > cur.idx) or (d.dma_key is not None and d.dma_cnt > cur.dma_cnt):
                deps[k] = d

        for r in reads:
            add(self.last_w.get(r))
        for w in writes:
            add(self.last_w.get(w))
            for rd in self.readers.get(w, ()):
                add(rd)
        o.deps = list(deps.values())
        o.idx = len(self.ops[eng])
        self.ops[eng].append(o)
        self.all_ops.append(o)
        if dma_key is not None:
            self.dma_count[dma_key] = self.dma_count.get(dma_key, 0) + 16
            o.dma_cnt = self.dma_count[dma_key]
        for r in reads:
            lst = self.readers.setdefault(r, [])
            if dma_key is None:
                lst[:] = [x for x in lst if not (x.dma_key is None and x.eng == eng)]
            lst.append(o)
        for w in writes:
            self.last_w[w] = o
            self.readers[w] = []
        return o

    def emit(self, nc, stack):
        for o in self.all_ops:
            for d in o.deps:
                if d.dma_key is None:
                    d.inc = True
        esem = {}
        for e in ENGS:
            esem[e] = stack.enter_context(nc.semaphore("s_" + e))
            c = 0
            for o in self.ops[e]:
                if o.dma_key is None and o.inc:
                    c += 1
                    o.semval = c
        dsem = {}
        for k in self.dma_count:
            dsem[k] = stack.enter_context(nc.semaphore("d%d" % len(dsem)))
        block = stack.enter_context(nc.Block())

        def run(e, h):
            waited = {}
            for o in self.ops[e]:
                for d in o.deps:
                    if d.dma_key is not None:
                        s, v = dsem[d.dma_key], d.dma_cnt
                    else:
                        s, v = esem[d.eng], d.semval
                    if waited.get(s.num, 0) >= v:
                        continue
                    waited[s.num] = v
                    h.wait_ge(s, v)
                if o.fn is None:
                    continue
                ins = o.fn(h)
                if o.dma_key is not None:
                    ins.then_inc(dsem[o.dma_key], 16)
                elif o.inc:
                    ins.then_inc(esem[e], 1)

        @block.tensor
        def _(h):
            run("pe", h)

        @block.scalar
        def _(h):
            run("act", h)

        @block.vector
        def _(h):
            run("dve", h)

        @block.gpsimd
        def _(h):
            run("pool", h)

        @block.sync
        def _(h):
            run("sp", h)


WSHAPES = {
    "ffn1_wg": [D, DFF], "ffn1_wu": [D, DFF], "ffn1_wd": [DFF, D],
    "w_in": [D, 4096], "pool_w": [4, 128, 128], "w_br_attn": [512, D], "w_br_pool": [512, D], "w_out": [D, D],
    "ffn2_wg": [D, DFF], "ffn2_wu": [D, DFF], "ffn2_wd": [DFF, D],
    "ple_wp": [256, D], "ple_wg": [D, D],
}


def build_program(nt_run=NT, stop_stage=99, debug=False):
    nc = bass.Bass("TRN2", target_bir_lowering=False)

    def din(name, shape):
        return nc.dram_tensor(name, shape, F32, kind="ExternalInput").ap()

    xT = din("xT", [D, SEQ])
    pT = din("pT", [256, SEQ])
    Wd = {k: din(k, s) for k, s in WSHAPES.items()}
    gains_d = din("gains", [128, 8, 8])
    pscale_d = din("pscale", [128, 4])
    biasT_d = din("biasT", [5, 128, 5120])
    invc_d = din("invc", [128, 4, 16])
    ident_d = din("ident", [128, 128])
    outT = nc.dram_tensor("outT", [D, SEQ], F32, kind="ExternalOutput").ap()

    dbg = {}
    if debug:
        for nm, shp in (("attn", [128, 4, T]), ("pooled", [128, 4, T]), ("yp2", [128, 4, T]), ("merged", [128, 8, T]), ("q", [128, 4, T]), ("k", [128, 4, T]), ("v", [128, 4, 512])):
            dbg[nm] = nc.dram_tensor("dbg_" + nm, shp, BF16, kind="ExternalOutput").ap()
    xTv = xT.rearrange("(c p) s -> p c s", p=128)
    outTv = outT.rearrange("(c p) s -> p c s", p=128)
    pTv = pT.rearrange("(c p) s -> p c s", p=128)

    def wview(name):
        return Wd[name].rearrange("(k p) n -> p k n", p=128)

    S = Sched()
    with ExitStack() as st:
        def sb(name, shape, dt):
            return st.enter_context(nc.sbuf_tensor(name, shape, dt))

        Hbuf = sb("Hbuf", [128, NH, 8, T], F32)
        bufF = sb("bufF", [128, 8 * T], F32)
        sq = sb("sq", [128, 4, T], BF16)
        xnm = sb("xnm", [128, 2, 8, T], BF16)
        hidA = sb("hidA", [128, NFC * T // 2], F32)
        kT = sb("kT", [128, 3, 4, T], BF16)
        Vr = sb("Vr", [128, 3, 4, 512], BF16)
        qT = sb("qT", [128, 2, 4, T], BF16)
        xp = sb("xp", [128, 3, 4, T + 16], BF16)
        PT = sb("PT", [128, 2, 1280], BF16)
        attnT = sb("attnT", [128, 2, 4, T], BF16)
        rden = sb("rden", [128, 2, 128], F32)
        biasI = sb("biasI", [128, 5120], BF16)
        sig = sb("sig", [128, 2, 2, T], F32)
        rstd = sb("rstd", [128, T], F32)
        tmp = sb("tmp", [128, 2, T], F32)
        pTb = sb("pTb", [128, 2, T], BF16)
        wring = [sb("w%d" % i, [128, WSLOT], BF16) for i in range(NW)]
        ident = sb("identb", [128, 128], BF16)
        ones = sb("onesb", [128, 128], BF16)
        gs = sb("gs", [128, 8, 8], F32)
        pscale = sb("pscale_sb", [128, 4], F32)
        epst = sb("epst", [128, 1], F32)
        invc = sb("invc_sb", [128, 4, 16], F32)
        etmp = sb("etmp", [128, 8], F32)
        ps = st.enter_context(nc.psum_tensor("ps", [128, 8 * 512], F32))

        def bank(b):
            return ps[:, b * 512:(b + 1) * 512]

        def bk(b):
            return [("ps", b, 0), ("ps", b, 1)]

        def pk(b):
            return bk(b)

        fF = bufF[:].rearrange("p (c t) -> p c t", c=8)
        bF16 = bufF[:].bitcast(BF16)
        xn = bF16[:, 0:8 * T].rearrange("p (c t) -> p c t", c=8)
        pooled = bF16[:, 8 * T:12 * T].rearrange("p (c t) -> p c t", c=4)
        yp2 = bF16[:, 12 * T:16 * T].rearrange("p (c t) -> p c t", c=4)
        hid = hidA[:].bitcast(BF16).rearrange("p (c t) -> p c t", c=NFC)
        fH = hidA[:, 0:8 * T].rearrange("p (c t) -> p c t", c=8)
        ptA = hidA[:, 8 * T:8 * T + 544]
        ptB = hidA[:, 8 * T + 768:8 * T + 768 + 544]

        def kF(c):
            return ("F", c)

        def kxn(c):
            return ("F", c // 2)

        def kfH(c):
            return [("hid", 2 * c), ("hid", 2 * c + 1)]

        KPTA = [("hid", 16), ("hid", 17), ("hid", 18)]
        KPTB = [("hid", 19), ("hid", 20), ("hid", 21)]

        S.op("sp", lambda h: h.dma_start(out=gs[:], in_=gains_d), writes=["gs"], dma_key="c_gs")
        S.op("sp", lambda h: h.dma_start(out=pscale[:], in_=pscale_d), writes=["pscale"], dma_key="c_ps")
        S.op("sp", lambda h: h.dma_start(out=invc[:], in_=invc_d), writes=["invc"], dma_key="c_ic")
        S.op("pool", lambda h: h.dma_start(out=ident[:], in_=ident_d), writes=["ident"], dma_key="c_id")
        S.op("pool", lambda h: h.dma_start(out=biasI[:], in_=biasT_d[0]), writes=["biasI"], dma_key="c_bi")
        S.op("dve", lambda h: h.memset(ones[:], 1.0), writes=["ones"])
        S.op("dve", lambda h: h.memset(epst[:], 1e-6), writes=["epst"])
        S.op("dve", lambda h: h.memset(xp[:, 0, :, 0:8], 0.0), writes=[("xpl", 0)])
        for v in (1, 5):
            S.op("dve", (lambda h, v=v: h.tensor_scalar(out=gs[:, v, :], in0=gs[:, v, :], scalar1=0.5, scalar2=None, op0=ALU.mult)),
                 reads=["gs"], writes=["gs"])

        wctr = [0]

        def wload(src, kc, ncols):
            i = wctr[0] % NW
            wctr[0] += 1
            dst = wring[i][:, 0:kc * ncols].rearrange("p (k n) -> p k n", k=kc)
            S.op("pool", lambda h: h.dma_start(out=dst, in_=src), writes=[("w", i)], dma_key=("w", i))
            return ("w", i), dst

        sqctr = [0]

        def share(n, c):
            return ((c + 1) * n) // 8 - (c * n) // 8

        def sq_act(src, keys, c):
            j = sqctr[0] % 4
            sqctr[0] += 1
            S.op("act", (lambda h, c=c, j=j: h.activation(out=sq[:, j, :], in_=src(c), func=AF.Square)),
                 reads=keys(c), writes=[("sq", j)])
            return j

        def ones_pe(c, j):
            S.op("pe", (lambda h, c=c, j=j: h.matmul(bank(PSN), lhsT=ones[:], rhs=sq[:, j, :], start=(c == 0), stop=(c == 7))),
                 reads=[("sq", j), "ones"], writes=bk(PSN))

        def rstd_finish():
            S.op("act", lambda h: h.activation(out=rstd[:], in_=bank(PSN), func=AF.Ln, bias=epst[:, 0:1], scale=1.0 / D),
                 reads=bk(PSN) + ["epst"], writes=["rstd"])
            S.op("act", lambda h: h.activation(out=rstd[:], in_=rstd[:], func=AF.Exp, scale=-0.5), reads=["rstd"], writes=["rstd"])

        def stats_loop(src, keys, ft, nf):
            for c in range(8):
                j = sq_act(src, keys, c)
                posts = fill_pe(ft, share(nf, c))
                ones_pe(c, j)
                for p in posts:
                    p()
            rstd_finish()

        def hkeys(t):
            return lambda c: [("h", t % NH, c)]

        def hsrc(t):
            return lambda c: Hbuf[:, t % NH, c, :]

        def apply_pre(t, v, dst, dkeys):
            for c in range(8):
                S.op("dve", (lambda h, c=c: h.scalar_tensor_tensor(out=dst(c), in0=Hbuf[:, t % NH, c, :], scalar=gs[:, v, c:c + 1],
                                                                     in1=rstd[:], op0=ALU.mult, op1=ALU.mult)),
                     reads=[("h", t % NH, c), "rstd", "gs"], writes=dkeys(c))

        def pre_only(t, v, dst, dkeys, ft=-1, nf=0):
            stats_loop(hsrc(t), hkeys(t), ft, nf)
            apply_pre(t, v, dst, dkeys)

        def boundary(t, vpost, src, keys, vpre=None, dst=None, dkeys=None, ft=-1, nf1=0, nf2=0):
            stats_loop(src, keys, ft, nf1)
            for c in range(8):
                i = c % 2
                S.op("dve", (lambda h, c=c, i=i: h.tensor_tensor(out=tmp[:, i, :], in0=src(c), in1=rstd[:], op=ALU.mult)),
                     reads=keys(c) + ["rstd"], writes=[("tmp", i)])
                S.op("dve", (lambda h, c=c, i=i: h.scalar_tensor_tensor(out=Hbuf[:, t % NH, c, :], in0=tmp[:, i, :], scalar=gs[:, vpost, c:c + 1],
                                                                          in1=Hbuf[:, t % NH, c, :], op0=ALU.mult, op1=ALU.add)),
                     reads=[("tmp", i), "gs", ("h", t % NH, c)], writes=[("h", t % NH, c)])
                j = sq_act(hsrc(t), hkeys(t), c) if vpre is not None else None
                posts = fill_pe(ft, share(nf2, c))
                if vpre is not None:
                    ones_pe(c, j)
                for p in posts:
                    p()
            if vpre is not None:
                rstd_finish()
                apply_pre(t, vpre, dst, dkeys)

        obank = [0]

        def next_obank():
            b = (4, 5, 7)[obank[0] % 3]
            obank[0] += 1
            return b

        def proj_fm(wname, col0, nchunks, rhs, rkeys, evac):
            wv = wview(wname)
            kc = wv.shape[1]
            for half in range(nchunks // 2):
                wk, wt = wload(wv[:, :, col0 + half * 256:col0 + (half + 1) * 256], kc, 256)
                for j in range(2):
                    ci = half * 2 + j
                    b = next_obank()
                    for k in range(kc):
                        S.op("pe", (lambda h, b=b, k=k, j=j, wt=wt: h.matmul(bank(b), lhsT=wt[:, k, j * 128:(j + 1) * 128], rhs=rhs(k),
                                                                              start=(k == 0), stop=(k == kc - 1))),
                             reads=[wk] + rkeys(k), writes=[*bk(b)])
                    evac(ci, b)

        def ffn_body(t, pfx):
            wg, wu, wd = wview(pfx + "_wg"), wview(pfx + "_wu"), wview(pfx + "_wd")
            for grp in range(NFC // 2):
                kg, tg = wload(wg[:, :, grp * 256:(grp + 1) * 256], 8, 256)
                ku, tu = wload(wu[:, :, grp * 256:(grp + 1) * 256], 8, 256)
                for j in range(2):
                    fc = 2 * grp + j
                    pb = (fc % 2) * 2
                    for (kk, tt, b) in ((kg, tg, pb), (ku, tu, pb + 1)):
                        for k in range(8):
                            S.op("pe", (lambda h, b=b, k=k, j=j, tt=tt: h.matmul(bank(b), lhsT=tt[:, k, j * 128:(j + 1) * 128], rhs=xn[:, k, :],
                                                                                  start=(k == 0), stop=(k == 7))),
                                 reads=[kk, kxn(k)], writes=[*bk(b)])
                    s = fc % 2
                    S.op("act", (lambda h, s=s, pb=pb: h.activation(out=sig[:, s, 0, :], in_=bank(pb), func=AF.Silu)),
                         reads=[*bk(pb)], writes=[("sig", s, 0)])
                    S.op("dve", (lambda h, s=s, pb=pb, fc=fc: h.tensor_tensor(out=hid[:, fc, :], in0=bank(pb + 1), in1=sig[:, s, 0, :], op=ALU.mult)),
                         reads=[*bk(pb + 1), ("sig", s, 0)], writes=[("hid", fc)])
            pieces = ((0, 8), (8, 16), (16, 22))
            for db in range(4):
                wl = [wload(wd[:, f0:f1, db * 256:(db + 1) * 256], f1 - f0, 256) for (f0, f1) in pieces]
                for j in range(2):
                    dc = 2 * db + j
                    b = 4 + dc % 2
                    for fc in range(NFC):
                        pi = 0 if fc < 8 else (1 if fc < 16 else 2)
                        wk, wt = wl[pi]
                        f0 = pieces[pi][0]
                        S.op("pe", (lambda h, b=b, fc=fc, f0=f0, j=j, wt=wt: h.matmul(bank(b), lhsT=wt[:, fc - f0, j * 128:(j + 1) * 128], rhs=hid[:, fc, :],
                                                                                     start=(fc == 0), stop=(fc == NFC - 1))),
                             reads=[wk, ("hid", fc)], writes=[*bk(b)])
                    S.op("act", (lambda h, b=b, dc=dc: h.activation(out=fF[:, dc, :], in_=bank(b), func=AF.Copy)),
                         reads=[*bk(b)], writes=[kF(dc)])

        def load_x(t):
            s = t % NH
            S.op("sp", lambda h: h.dma_start(out=Hbuf[:, s, :, :], in_=xTv[:, :, t * T:(t + 1) * T]),
                 writes=[("h", s, c) for c in range(8)], dma_key=("x", s))

        out_ops = []

        def store_h(t):
            s = t % NH
            S.op("sp", lambda h: h.dma_start(out=outTv[:, :, t * T:(t + 1) * T], in_=Hbuf[:, s, :, :]),
                 reads=[("h", s, c) for c in range(8)], writes=[("outd", s)], dma_key=("o", s))

        def stage_A(t):
            load_x(t)
            m = t % 2
            s3 = t % 3
            pre_only(t, 0, lambda c: xn[:, c, :], lambda c: [kxn(c)], ft=t - 1, nf=FA[0])
            ffn_body(t, "ffn1")
            boundary(t, 1, lambda c: fF[:, c, :], lambda c: [kF(c)], 2, lambda c: xnm[:, m, c, :], lambda c: [("xm", m, c)],
                     ft=t - 1, nf1=FA[1], nf2=FA[2])
            if stop_stage <= 1:
                return
            rhs = lambda k: xnm[:, m, k, :]
            rk = lambda k: [("xm", m, k)]

            def evac_k(ci, b):
                S.op("act", lambda h: h.activation(out=kT[:, s3, ci, :], in_=bank(b), func=AF.Copy),
                     reads=[*bk(b)], writes=[("kT", s3, ci)])

            proj_fm("w_in", 512, 4, rhs, rk, evac_k)

            def evac_xp(ci, b):
                S.op("dve", lambda h: h.tensor_copy(out=xp[:, s3, ci, 8:8 + T], in_=bank(b)),
                     reads=[*bk(b)], writes=[("xpc", s3, ci)])
                S.op("dve", lambda h: h.tensor_copy(out=xp[:, (s3 + 2) % 3, ci, 8 + T:16 + T], in_=bank(b)[:, 0:8]),
                     reads=[*bk(b)], writes=[("xpr", (s3 + 2) % 3, ci)])
                S.op("dve", lambda h: h.tensor_copy(out=xp[:, (s3 + 1) % 3, ci, 0:8], in_=bank(b)[:, T - 8:T]),
                     reads=[*bk(b)], writes=[("xpl", (s3 + 1) % 3, ci)])
                if t == NT - 1:
                    S.op("dve", lambda h: h.memset(xp[:, s3, ci, 8 + T:16 + T], 0.0), writes=[("xpr", s3, ci)])

            proj_fm("w_in", 1536, 4, rhs, rk, evac_xp)
            wv = wview("w_in")
            for piece in range(2):
                wk, wt = wload(wv[:, :, 1024 + piece * 256:1024 + (piece + 1) * 256], 8, 256)
                for blk in range(4):
                    b = next_obank()
                    for k in range(8):
                        S.op("pe", (lambda h, b=b, k=k, blk=blk, wt=wt: h.matmul(bank(b)[:, 0:256], lhsT=xnm[:, m, k, blk * 128:(blk + 1) * 128], rhs=wt[:, k, :],
                                                                                  start=(k == 0), stop=(k == 7))),
                             reads=[wk, ("xm", m, k)], writes=[*bk(b)])
                    S.op("dve", (lambda h, b=b, blk=blk, piece=piece: h.tensor_copy(out=Vr[:, s3, blk, piece * 256:(piece + 1) * 256], in_=bank(b)[:, 0:256])),
                         reads=[*bk(b)], writes=[("V", s3, blk, piece)])

        def xpkeys(s3, g):
            return [("xpc", s3, g), ("xpr", s3, g), ("xpl", s3, g), ("xpl", s3)]

        def emit_qproj(t):
            m = t % 2
            qdone[t] = True

            def evac_q(ci, b):
                S.op("act", lambda h: h.mul(out=qT[:, m, ci, :], in_=bank(b), mul=0.125), reads=[*bk(b)], writes=[("q", m, ci)])

            proj_fm("w_in", 0, 4, lambda k: xnm[:, m, k, :], lambda k: [("xm", m, k)], evac_q)

        ITEMS = [(rp, c) for rp in range(4) for c in range(4)]
        anext = [0] * (NT + 1)
        qdone = [False] * (NT + 1)
        apar = [0]
        apend = [None]

        def geom(t, rp):
            r = 8 * t + 2 * rp
            kb = min(max(r - 4, 0), 54)
            var = {0: 1, 2: 2, 60: 3, 62: 4}.get(r, 0)
            return r, kb, var

        def item_steps(t, i, sbi):
            rp, c = ITEMS[i]
            r, kb, var = geom(t, rp)
            m = t % 2
            st = {}

            def bias_view(hh, j):
                if var == 0:
                    return ["biasI"], biasI[:, ((2 * c + hh) * 5 + j) * 128:((2 * c + hh) * 5 + j + 1) * 128]
                if "w" not in st:
                    st["w"] = wload(biasT_d[var][:, c * 1280:(c + 1) * 1280].rearrange("p (k n) -> p k n", k=1), 1, 1280)
                wk, wt = st["w"]
                return [wk], wt[:, 0, (hh * 5 + j) * 128:(hh * 5 + j + 1) * 128]

            def s_pe(hh):
                for j in range(5):
                    krow = kb + 2 * j
                    ks = (krow // 8) % 3
                    koff = (krow % 8) * 64
                    if j < 4:
                        bnk = sbi * 3 + hh
                        out = bank(bnk)[:, j * 128:(j + 1) * 128]
                    else:
                        bnk = sbi * 3 + 2
                        out = bank(bnk)[:, hh * 128:(hh + 1) * 128]
                    bkeys, bv = bias_view(hh, j)
                    S.op("pe", (lambda h, out=out, ks=ks, koff=koff: h.matmul(out, lhsT=kT[64 * hh:64 * hh + 64, ks, c, koff:koff + 128],
                                                                            rhs=qT[64 * hh:64 * hh + 64, m, c, rp * 128:(rp + 1) * 128], start=True, stop=False)),
                         reads=[("kT", ks, c), ("q", m, c)], writes=bk(bnk))
                    S.op("pe", (lambda h, out=out, bv=bv: h.matmul(out, lhsT=ident[:], rhs=bv, start=False, stop=True)),
                         reads=["ident"] + bkeys, writes=bk(bnk))

            def s_post(hh):
                S.op("act", lambda h: h.activation(out=PT[:, sbi, hh * 512:(hh + 1) * 512], in_=bank(sbi * 3 + hh)[:, 0:512], func=AF.Exp),
                     reads=bk(sbi * 3 + hh), writes=[("PT", sbi, hh * 512)])
                S.op("act", lambda h: h.activation(out=PT[:, sbi, 1024 + hh * 128:1152 + hh * 128], in_=bank(sbi * 3 + 2)[:, hh * 128:(hh + 1) * 128], func=AF.Exp),
                     reads=bk(sbi * 3 + 2), writes=[("PT", sbi, 1024 + hh * 128)])

            nd = bank(3 * sbi + 2)[:, 256:512]

            ndk = bk(3 * sbi + 2)

            def pv_pe(_):
                for hh in range(2):
                    hd = 2 * c + hh
                    for which in range(2):
                        for j in range(5):
                            krow = kb + 2 * j
                            ks = (krow // 8) % 3
                            blk = (krow % 8) // 2
                            c0 = (hh * 512 + j * 128) if j < 4 else (1024 + hh * 128)
                            pk_ = ("PT", sbi, c0 if c0 >= 1024 else (c0 // 512) * 512)
                            if which == 0:
                                S.op("pe", (lambda h, hh=hh, hd=hd, ks=ks, blk=blk, c0=c0, j=j: h.matmul(nd[64 * hh:64 * hh + 64, 0:128], lhsT=Vr[:, ks, blk, hd * 64:(hd + 1) * 64],
                                                                                                       rhs=PT[:, sbi, c0:c0 + 128], start=(j == 0), stop=(j == 4))),
                                     reads=[("V", ks, blk, hd // 4), pk_], writes=ndk)
                            else:
                                S.op("pe", (lambda h, hh=hh, c0=c0, j=j: h.matmul(nd[64 * hh:64 * hh + 64, 128:256], lhsT=ones[:, 0:64],
                                                                                   rhs=PT[:, sbi, c0:c0 + 128], start=(j == 0), stop=(j == 4))),
                                     reads=["ones", pk_], writes=ndk)

            def pv_post(_):
                S.op("act", lambda h: h.activation(out=rden[:, sbi, :], in_=nd[:, 128:256], func=AF.Ln), reads=ndk, writes=[("rden", sbi)])
                S.op("act", lambda h: h.activation(out=rden[:, sbi, :], in_=rden[:, sbi, :], func=AF.Exp, scale=-1.0),
                     reads=[("rden", sbi)], writes=[("rden", sbi)])
                S.op("dve", lambda h: h.tensor_tensor(out=attnT[:, m, c, rp * 128:(rp + 1) * 128], in0=nd[:, 0:128], in1=rden[:, sbi, :], op=ALU.mult),
                     reads=ndk + [("rden", sbi)], writes=[("attn", m, c)])

            mk = lambda f, g, hh: ((lambda: f(hh)), (lambda: g(hh)))
            return [mk(s_pe, s_post, 0), mk(s_pe, s_post, 1)], [mk(pv_pe, pv_post, 0)]

        aqueue = [[] for _ in range(NT + 1)]
        apv = [None] * (NT + 1)

        def attn_pop(t, limit):
            if t < 0 or t >= NT or not qdone[t]:
                return None
            q = aqueue[t]
            if not q:
                if anext[t] < limit:
                    i = anext[t]
                    anext[t] += 1
                    sbi = apar[0] % 2
                    apar[0] += 1
                    ss, pv = item_steps(t, i, sbi)
                    q.extend(ss)
                    if apv[t] is not None:
                        q.extend(apv[t])
                    apv[t] = pv
                elif apv[t] is not None and limit >= 16:
                    q.extend(apv[t])
                    apv[t] = None
            if not q:
                return None
            return q.pop(0)

        def fill_pe(t, n):
            posts = []
            if FILL:
                for _ in range(n):
                    stp = attn_pop(t, 8)
                    if stp is None:
                        break
                    stp[0]()
                    posts.append(stp[1])
            return posts

        def fill_steps(t, n):
            for p in fill_pe(t, n):
                p()

        def attn_run(t):
            while True:
                stp = attn_pop(t, 16)
                if stp is None:
                    break
                stp[0]()
                stp[1]()

        def stage_B(t):
            m = t % 2
            s3 = t % 3
            rhs = lambda k: xnm[:, m, k, :]
            rk = lambda k: [("xm", m, k)]
            if t == 0:
                emit_qproj(0)
            if t + 1 < NT:
                emit_qproj(t + 1)
            for g in range(4):
                w = (2, 4, 8, 16)[g]
                L = T + 16
                X = xp[:, s3, g, :]
                S.op("dve", (lambda h, X=X, L=L: h.tensor_tensor(out=ptA[:, 1:L], in0=X[:, 0:L - 1], in1=X[:, 1:L], op=ALU.add)),
                     reads=xpkeys(s3, g), writes=KPTA)
                cur, curk, oth, othk = ptA, KPTA, ptB, KPTB
                lo, hi = 1, L
                sh = 1
                while sh * 2 < w:
                    nlo, nhi = lo + sh, hi - sh
                    S.op("dve", (lambda h, cur=cur, oth=oth, nlo=nlo, nhi=nhi, sh=sh: h.tensor_tensor(out=oth[:, nlo:nhi], in0=cur[:, nlo - sh:nhi - sh],
                                                                                                       in1=cur[:, nlo + sh:nhi + sh], op=ALU.add)),
                         reads=curk, writes=othk)
                    cur, curk, oth, othk = oth, othk, cur, curk
                    lo, hi = nlo, nhi
                    sh *= 2
                assert lo <= 8 and hi >= 8 + T
                S.op("dve", (lambda h, cur=cur, g=g, w=w, X=X: h.scalar_tensor_tensor(out=pooled[:, g, :], in0=cur[:, 8:8 + T], scalar=1.0 / w, in1=X[:, 8:8 + T],
                                                                                 op0=ALU.mult, op1=ALU.subtract)),
                     reads=curk + xpkeys(s3, g), writes=[("F", 4 + g // 2)])
                for (cond, off, io) in ((t == 0, 0, 0), (t == NT - 1, T - 8, 8)):
                    if cond:
                        S.op("dve", (lambda h, cur=cur, g=g, off=off, io=io: h.tensor_tensor(out=etmp[:], in0=cur[:, 8 + off:16 + off], in1=invc[:, g, io:io + 8], op=ALU.mult)),
                             reads=curk + ["invc"], writes=["etmp"])
                        S.op("dve", (lambda h, g=g, off=off, X=X: h.tensor_tensor(out=pooled[:, g, off:off + 8], in0=etmp[:], in1=X[:, 8 + off:16 + off], op=ALU.subtract)),
                             reads=["etmp"] + xpkeys(s3, g), writes=[("F", 4 + g // 2)])
            pwk, pwt = wload(Wd["pool_w"].rearrange("g c d -> c g d"), 4, 128)
            for g in range(4):
                b = next_obank()
                S.op("pe", (lambda h, b=b, g=g: h.matmul(bank(b), lhsT=pwt[:, g, :], rhs=pooled[:, g, :], start=True, stop=True)),
                     reads=[pwk, ("F", 4 + g // 2)], writes=[*bk(b)])
                S.op("dve", (lambda h, b=b, g=g: h.tensor_scalar(out=yp2[:, g, :], in0=bank(b), scalar1=pscale[:, g:g + 1], scalar2=None, op0=ALU.mult)),
                     reads=[*bk(b), "pscale"], writes=[("F", 6 + g // 2)])

            attn_run(t)


            wba, wbp, win = wview("w_br_attn"), wview("w_br_pool"), wview("w_in")
            for db in range(4):
                ka, ta = wload(wba[:, :, db * 256:(db + 1) * 256], 4, 256)
                kp, tp = wload(wbp[:, :, db * 256:(db + 1) * 256], 4, 256)
                kga, tga = wload(win[:, :, 2048 + db * 256:2048 + (db + 1) * 256], 8, 256)
                kgp, tgp = wload(win[:, :, 3072 + db * 256:3072 + (db + 1) * 256], 8, 256)
                for j in range(2):
                    dc = 2 * db + j
                    s = dc % 2
                    b0 = s * 4
                    for k in range(4):
                        S.op("pe", (lambda h, k=k, j=j, b0=b0, ta=ta: h.matmul(bank(b0), lhsT=ta[:, k, j * 128:(j + 1) * 128], rhs=attnT[:, m, k, :], start=(k == 0), stop=(k == 3))),
                             reads=[ka, ("attn", m, k)], writes=pk(b0))
                    for k in range(4):
                        S.op("pe", (lambda h, k=k, j=j, b0=b0, tp=tp: h.matmul(bank(b0 + 1), lhsT=tp[:, k, j * 128:(j + 1) * 128], rhs=yp2[:, k, :], start=(k == 0), stop=(k == 3))),
                             reads=[kp, ("F", 6 + k // 2)], writes=pk(b0 + 1))
                    for (kk, tt, bo) in ((kga, tga, 2), (kgp, tgp, 3)):
                        for k in range(8):
                            S.op("pe", (lambda h, k=k, j=j, b0=b0, tt=tt, bo=bo: h.matmul(bank(b0 + bo), lhsT=tt[:, k, j * 128:(j + 1) * 128], rhs=xnm[:, m, k, :],
                                                                                       start=(k == 0), stop=(k == 7))),
                                 reads=[kk, ("xm", m, k)], writes=pk(b0 + bo))
                    for q in range(2):
                        S.op("act", (lambda h, q=q, s=s, b0=b0: h.activation(out=sig[:, s, q, :], in_=bank(b0 + 2 + q), func=AF.Sigmoid)),
                             reads=pk(b0 + 2 + q), writes=[("sig", s, q)])
                        S.op("dve", (lambda h, q=q, s=s, b0=b0: h.tensor_tensor(out=sig[:, s, q, :], in0=bank(b0 + q), in1=sig[:, s, q, :], op=ALU.mult)),
                             reads=pk(b0 + q) + [("sig", s, q)], writes=[("sig", s, q)])
                    S.op("dve", (lambda h, s=s, dc=dc: h.tensor_tensor(out=xn[:, dc, :], in0=sig[:, s, 0, :], in1=sig[:, s, 1, :], op=ALU.add)),
                         reads=[("sig", s, 0), ("sig", s, 1)], writes=[kxn(dc)])

            def evac_m(ci, b):
                S.op("act", lambda h: h.activation(out=fH[:, ci, :], in_=bank(b), func=AF.Copy), reads=[*bk(b)], writes=kfH(ci))

            if debug and t == int(os.environ.get("K_DBG_T", "0")):
                def dump(nm, src, keys):
                    S.op("sp", lambda h: h.dma_start(out=dbg[nm], in_=src), reads=keys, writes=[("dbg", nm)], dma_key=("dbg", nm))
                dump("attn", attnT[:, m], [("attn", m, c) for c in range(4)])
                dump("pooled", pooled, [("F", 4), ("F", 5)])
                dump("yp2", yp2, [("F", 6), ("F", 7)])
                dump("merged", xn, [("F", c) for c in range(4)])
                dump("q", qT[:, m], [("q", m, c) for c in range(4)])
                dump("k", kT[:, 0, :, :], [("kT", 0, c) for c in range(4)])
                dump("v", Vr[:, 0, :, :], [("V", 0, b, p_) for b in range(4) for p_ in range(2)])
            proj_fm("w_out", 0, 8, lambda k: xn[:, k, :], lambda k: [kxn(k)], evac_m)
            boundary(t, 3, lambda c: fH[:, c, :], kfH, 4, lambda c: xn[:, c, :], lambda c: [kxn(c)], ft=t + 1, nf1=FB[0], nf2=FB[1])
            ffn_body(t, "ffn2")
            boundary(t, 5, lambda c: fF[:, c, :], lambda c: [kF(c)], 6, lambda c: xn[:, c, :], lambda c: [kxn(c)], ft=t + 1, nf1=FB[2], nf2=FB[3])
            S.op("pool", lambda h: h.dma_start(out=pTb[:], in_=pTv[:, :, t * T:(t + 1) * T]), writes=["pTb"], dma_key="pTb")
            wp, wgt = wview("ple_wp"), wview("ple_wg")
            for db in range(4):
                kp_, tp_ = wload(wp[:, :, db * 256:(db + 1) * 256], 2, 256)
                kg_, tg_ = wload(wgt[:, :, db * 256:(db + 1) * 256], 8, 256)
                for j in range(2):
                    dc = 2 * db + j
                    s = dc % 2
                    b0 = s * 2
                    for k in range(2):
                        S.op("pe", (lambda h, k=k, j=j, b0=b0, tp_=tp_: h.matmul(bank(b0), lhsT=tp_[:, k, j * 128:(j + 1) * 128], rhs=pTb[:, k, :], start=(k == 0), stop=(k == 1))),
                             reads=[kp_, "pTb"], writes=[*bk(b0)])
                    for k in range(8):
                        S.op("pe", (lambda h, k=k, j=j, b0=b0, tg_=tg_: h.matmul(bank(b0 + 1), lhsT=tg_[:, k, j * 128:(j + 1) * 128], rhs=xn[:, k, :], start=(k == 0), stop=(k == 7))),
                             reads=[kg_, kxn(k)], writes=[*bk(b0 + 1)])
                    S.op("act", (lambda h, s=s, b0=b0: h.activation(out=sig[:, s, 0, :], in_=bank(b0 + 1), func=AF.Sigmoid)),
                         reads=[*bk(b0 + 1)], writes=[("sig", s, 0)])
                    S.op("dve", (lambda h, s=s, b0=b0, dc=dc: h.tensor_tensor(out=fH[:, dc, :], in0=bank(b0), in1=sig[:, s, 0, :], op=ALU.mult)),
                         reads=[*bk(b0), ("sig", s, 0)], writes=kfH(dc))
            boundary(t, 7, lambda c: fH[:, c, :], kfH, ft=t + 1, nf1=FB[4], nf2=FB[5])
            store_h(t)

        if stop_stage <= 1:
            for t in range(nt_run):
                stage_A(t)
                store_h(t)
        else:
            stage_A(0)
            for t in range(nt_run):
                if t + 1 < NT:
                    stage_A(t + 1)
                stage_B(t)
        S.op("sp", None, reads=[("outd", s) for s in range(NH)] + [("dbg", nm) for nm in dbg])
        S.emit(nc, st)
    return nc


def _bias_table(rpb):
    out = np.full((5, 128, 8, 5, 128), NEG, dtype=np.float32)
    kl, kc = np.divmod(np.arange(128), 64)
    ql, qc = np.divmod(np.arange(128), 64)
    cs = np.clip(qc - 8, 0, 48)
    for var, r in enumerate((4, 0, 2, 60, 62)):
        kb = min(max(r - 4, 0), 54)
        for j in range(5):
            key_row = kb + 2 * j + kl
            q_row = r + ql
            rs = np.clip(q_row - 4, 0, 56)
            vr = (key_row[:, None] >= rs[None, :]) & (key_row[:, None] < rs[None, :] + 8)
            vc = (kc[:, None] >= cs[None, :]) & (kc[:, None] < cs[None, :] + 16)
            dr = np.clip(key_row[:, None] - q_row[None, :] + 7, 0, 14)
            dcc = np.clip(kc[:, None] - qc[None, :] + 15, 0, 30)
            valid = vr & vc
            for h in range(8):
                out[var, :, h, j, :] = np.where(valid, rpb[h][dr, dcc], np.float32(NEG))
    return out.reshape(5, 128, 5120)


def _chunked(v):
    return np.ascontiguousarray(v.reshape(-1, 128).T)


def prepare_inputs(inputs):
    g = lambda k: np.asarray(inputs[k], dtype=np.float32)
    shared = {
        "ffn1_wg": g("ffn1_w_gate")[0], "ffn1_wu": g("ffn1_w_up")[0], "ffn1_wd": g("ffn1_w_down")[0],
        "w_in": g("w_in")[0], "pool_w": g("pool_w")[0], "w_br_attn": g("w_br_attn")[0], "w_br_pool": g("w_br_pool")[0],
        "w_out": g("w_out")[0],
        "ffn2_wg": g("ffn2_w_gate")[0], "ffn2_wu": g("ffn2_w_up")[0], "ffn2_wd": g("ffn2_w_down")[0],
        "ple_wp": g("ple_w_proj")[0], "ple_wg": g("ple_w_gate")[0],
    }
    names = ["ffn1_pre_g", "ffn1_post_g", "mix_pre_g", "mix_post_g", "ffn2_pre_g", "ffn2_post_g", "ple_pre_g", "ple_post_g"]
    shared["gains"] = np.ascontiguousarray(np.stack([_chunked(g(n)[0]) for n in names], axis=1))
    shared["pscale"] = _chunked(g("pool_scale")[0])
    shared["biasT"] = _bias_table(g("rpb")[0])
    invc = np.zeros((128, 4, 16), np.float32)
    for gi, w in enumerate((2, 4, 8, 16)):
        half = w // 2
        for i in range(8):
            tl = i
            tr = SEQ - 8 + i
            invc[:, gi, i] = 1.0 / (min(tl + half, SEQ) - max(tl - half, 0))
            invc[:, gi, 8 + i] = 1.0 / (min(tr + half, SEQ) - max(tr - half, 0))
    shared["invc"] = invc
    shared["ident"] = np.eye(128, dtype=np.float32)
    shared = {k: np.ascontiguousarray(v) for k, v in shared.items()}
    x = g("x")
    p = g("p")[0]
    maps = []
    for b in range(x.shape[0]):
        m = dict(shared)
        m["xT"] = np.ascontiguousarray(x[b].T)
        m["pT"] = np.ascontiguousarray(p[b].T)
        maps.append(m)
    return maps


_NC_CACHE = {}


def kernel(**inputs):
    maps = prepare_inputs(inputs)
    if "nc" not in _NC_CACHE:
        _NC_CACHE["nc"] = build_program()
    nc = _NC_CACHE["nc"]
    res = run_bass_kernel_spmd(nc, maps, core_ids=list(range(8)))
    out = np.stack([np.ascontiguousarray(r["outT"].T) for r in res.results], axis=0)
    return out.astype(np.float32)
```

```python
import os
import numpy as np
from contextlib import ExitStack
import concourse.bass as bass
import concourse.mybir as mybir
from concourse.bass_utils import run_bass_kernel_spmd

F32 = mybir.dt.float32
BF16 = mybir.dt.bfloat16
AF = mybir.ActivationFunctionType
ALU = mybir.AluOpType

D = 1024
SEQ = 4096
T = 512
NT = SEQ // T
DFF = 2816
NFC = DFF // 128
NEG = -30000.0
NH = 2
NW = 7
FILL = os.environ.get("K_FILL", "1") == "1"
FB = tuple(int(v) for v in os.environ.get("K_FB", "0,4,0,4,0,4").split(","))
FA = tuple(int(v) for v in os.environ.get("K_FA", "0,0,4").split(","))
WSLOT = 2048
PSN = 6


class Op:
    __slots__ = ("eng", "fn", "deps", "idx", "inc", "semval", "dma_key", "dma_cnt")

    def __init__(self, eng, fn, dma_key=None):
        self.eng = eng
        self.fn = fn
        self.deps = []
        self.idx = -1
        self.inc = False
        self.semval = 0
        self.dma_key = dma_key
        self.dma_cnt = 0


ENGS = ("pe", "act", "dve", "pool", "sp")


class Sched:
    def __init__(self):
        self.ops = {e: [] for e in ENGS}
        self.last_w = {}
        self.readers = {}
        self.dma_count = {}
        self.all_ops = []

    def op(self, eng, fn, reads=(), writes=(), dma_key=None):
        o = Op(eng, fn, dma_key)
        deps = {}

        def add(d):
            if d is None:
                return
            if d.dma_key is None and d.eng == "pe" and eng == "pe" and dma_key is None:
                return
            k = ("dma", d.dma_key) if d.dma_key is not None else ("eng", d.eng)
            cur = deps.get(k)
            if cur is None or (d.dma_key is None and d.idx > cur.idx) or (d.dma_key is not None and d.dma_cnt > cur.dma_cnt):
                deps[k] = d

        for r in reads:
            add(self.last_w.get(r))
        for w in writes:
            add(self.last_w.get(w))
            for rd in self.readers.get(w, ()):
                add(rd)
        o.deps = list(deps.values())
        o.idx = len(self.ops[eng])
        self.ops[eng].append(o)
        self.all_ops.append(o)
        if dma_key is not None:
            self.dma_count[dma_key] = self.dma_count.get(dma_key, 0) + 16
            o.dma_cnt = self.dma_count[dma_key]
        for r in reads:
            lst = self.readers.setdefault(r, [])
            if dma_key is None:
                lst[:] = [x for x in lst if not (x.dma_key is None and x.eng == eng)]
            lst.append(o)
        for w in writes:
            self.last_w[w] = o
            self.readers[w] = []
        return o

    def emit(self, nc, stack):
        for o in self.all_ops:
            for d in o.deps:
                if d.dma_key is None:
                    d.inc = True
        esem = {}
        for e in ENGS:
            esem[e] = stack.enter_context(nc.semaphore("s_" + e))
            c = 0
            for o in self.ops[e]:
                if o.dma_key is None and o.inc:
                    c += 1
                    o.semval = c
        dsem = {}
        for k in self.dma_count:
            dsem[k] = stack.enter_context(nc.semaphore("d%d" % len(dsem)))
        block = stack.enter_context(nc.Block())

        def run(e, h):
            waited = {}
            for o in self.ops[e]:
                for d in o.deps:
                    if d.dma_key is not None:
                        s, v = dsem[d.dma_key], d.dma_cnt
                    else:
                        s, v = esem[d.eng], d.semval
                    if waited.get(s.num, 0) >= v:
                        continue
                    waited[s.num] = v
                    h.wait_ge(s, v)
                if o.fn is None:
                    continue
                ins = o.fn(h)
                if o.dma_key is not None:
                    ins.then_inc(dsem[o.dma_key], 16)
                elif o.inc:
                    ins.then_inc(esem[e], 1)

        @block.tensor
        def _(h):
            run("pe", h)

        @block.scalar
        def _(h):
            run("act", h)

        @block.vector
        def _(h):
            run("dve", h)

        @block.gpsimd
        def _(h):
            run("pool", h)

        @block.sync
        def _(h):
            run("sp", h)


WSHAPES = {
    "ffn1_wg": [D, DFF], "ffn1_wu": [D, DFF], "ffn1_wd": [DFF, D],
    "w_in": [D, 4096], "pool_w": [4, 128, 128], "w_br_attn": [512, D], "w_br_pool": [512, D], "w_out": [D, D],
    "ffn2_wg": [D, DFF], "ffn2_wu": [D, DFF], "ffn2_wd": [DFF, D],
    "ple_wp": [256, D], "ple_wg": [D, D],
}


def build_program(nt_run=NT, stop_stage=99, debug=False):
    nc = bass.Bass("TRN2", target_bir_lowering=False)

    def din(name, shape):
        return nc.dram_tensor(name, shape, F32, kind="ExternalInput").ap()

    xT = din("xT", [D, SEQ])
    pT = din("pT", [256, SEQ])
    Wd = {k: din(k, s) for k, s in WSHAPES.items()}
    gains_d = din("gains", [128, 8, 8])
    pscale_d = din("pscale", [128, 4])
    biasT_d = din("biasT", [5, 128, 5120])
    invc_d = din("invc", [128, 4, 16])
    ident_d = din("ident", [128, 128])
    outT = nc.dram_tensor("outT", [D, SEQ], F32, kind="ExternalOutput").ap()

    dbg = {}
    if debug:
        for nm, shp in (("attn", [128, 4, T]), ("pooled", [128, 4, T]), ("yp2", [128, 4, T]), ("merged", [128, 8, T]), ("q", [128, 4, T]), ("k", [128, 4, T]), ("v", [128, 4, 512])):
            dbg[nm] = nc.dram_tensor("dbg_" + nm, shp, BF16, kind="ExternalOutput").ap()
    xTv = xT.rearrange("(c p) s -> p c s", p=128)
    outTv = outT.rearrange("(c p) s -> p c s", p=128)
    pTv = pT.rearrange("(c p) s -> p c s", p=128)

    def wview(name):
        return Wd[name].rearrange("(k p) n -> p k n", p=128)

    S = Sched()
    with ExitStack() as st:
        def sb(name, shape, dt):
            return st.enter_context(nc.sbuf_tensor(name, shape, dt))

        Hbuf = sb("Hbuf", [128, NH, 8, T], F32)
        bufF = sb("bufF", [128, 8 * T], F32)
        sq = sb("sq", [128, 4, T], BF16)
        xnm = sb("xnm", [128, 2, 8, T], BF16)
        hidA = sb("hidA", [128, NFC * T // 2], F32)
        kT = sb("kT", [128, 3, 4, T], BF16)
        Vr = sb("Vr", [128, 3, 4, 512], BF16)
        qT = sb("qT", [128, 2, 4, T], BF16)
        xp = sb("xp", [128, 3, 4, T + 16], BF16)
        PT = sb("PT", [128, 2, 1280], BF16)
        attnT = sb("attnT", [128, 2, 4, T], BF16)
        rden = sb("rden", [128, 2, 128], F32)
        numS = sb("numS", [128, 2, 128], F32)
        biasI = sb("biasI", [128, 5120], BF16)
        sig = sb("sig", [128, 2, 2, T], F32)
        rstd = sb("rstd", [128, T], F32)
        tmp = sb("tmp", [128, 2, T], F32)
        pTb = sb("pTb", [128, 2, T], BF16)
        wring = [sb("w%d" % i, [128, WSLOT], BF16) for i in range(NW)]
        ident = sb("identb", [128, 128], BF16)
        ones = sb("onesb", [128, 128], BF16)
        gs = sb("gs", [128, 8, 8], F32)
        pscale = sb("pscale_sb", [128, 4], F32)
        epst = sb("epst", [128, 1], F32)
        invc = sb("invc_sb", [128, 4, 16], F32)
        etmp = sb("etmp", [128, 8], F32)
        ps = st.enter_context(nc.psum_tensor("ps", [128, 8 * 512], F32))

        def bank(b):
            return ps[:, b * 512:(b + 1) * 512]

        def bk(b):
            return [("ps", b, 0), ("ps", b, 1)]

        def pk(b):
            return bk(b)

        fF = bufF[:].rearrange("p (c t) -> p c t", c=8)
        bF16 = bufF[:].bitcast(BF16)
        xn = bF16[:, 0:8 * T].rearrange("p (c t) -> p c t", c=8)
        pooled = bF16[:, 8 * T:12 * T].rearrange("p (c t) -> p c t", c=4)
        yp2 = bF16[:, 12 * T:16 * T].rearrange("p (c t) -> p c t", c=4)
        hid = hidA[:].bitcast(BF16).rearrange("p (c t) -> p c t", c=NFC)
        fH = hidA[:, 0:8 * T].rearrange("p (c t) -> p c t", c=8)
        ptA = hidA[:, 8 * T:8 * T + 544]
        ptB = hidA[:, 8 * T + 768:8 * T + 768 + 544]

        def kF(c):
            return ("F", c)

        def kxn(c):
            return ("F", c // 2)

        def kfH(c):
            return [("hid", 2 * c), ("hid", 2 * c + 1)]

        KPTA = [("hid", 16), ("hid", 17), ("hid", 18)]
        KPTB = [("hid", 19), ("hid", 20), ("hid", 21)]

        S.op("sp", lambda h: h.dma_start(out=gs[:], in_=gains_d), writes=["gs"], dma_key="c_gs")
        S.op("sp", lambda h: h.dma_start(out=pscale[:], in_=pscale_d), writes=["pscale"], dma_key="c_ps")
        S.op("sp", lambda h: h.dma_start(out=invc[:], in_=invc_d), writes=["invc"], dma_key="c_ic")
        S.op("pool", lambda h: h.dma_start(out=ident[:], in_=ident_d), writes=["ident"], dma_key="c_id")
        S.op("pool", lambda h: h.dma_start(out=biasI[:], in_=biasT_d[0]), writes=["biasI"], dma_key="c_bi")
        for q4 in range(4):
            S.op("act", (lambda h, q4=q4: h.activation(out=biasI[:, q4 * 1280:(q4 + 1) * 1280], in_=biasI[:, q4 * 1280:(q4 + 1) * 1280], func=AF.Exp)),
                 reads=["biasI"], writes=["biasI"])
        S.op("dve", lambda h: h.memset(ones[:], 1.0), writes=["ones"])
        S.op("dve", lambda h: h.memset(epst[:], 1e-6), writes=["epst"])
        S.op("dve", lambda h: h.memset(xp[:, 0, :, 0:8], 0.0), writes=[("xpl", 0)])
        for v in (1, 5):
            S.op("dve", (lambda h, v=v: h.tensor_scalar(out=gs[:, v, :], in0=gs[:, v, :], scalar1=0.5, scalar2=None, op0=ALU.mult)),
                 reads=["gs"], writes=["gs"])

        wctr = [0]

        def wload(src, kc, ncols):
            i = wctr[0] % NW
            wctr[0] += 1
            dst = wring[i][:, 0:kc * ncols].rearrange("p (k n) -> p k n", k=kc)
            S.op("pool", lambda h: h.dma_start(out=dst, in_=src), writes=[("w", i)], dma_key=("w", i))
            return ("w", i), dst

        sqctr = [0]

        def share(n, c):
            return ((c + 1) * n) // 8 - (c * n) // 8

        def sq_act(src, keys, c):
            j = sqctr[0] % 4
            sqctr[0] += 1
            S.op("act", (lambda h, c=c, j=j: h.activation(out=sq[:, j, :], in_=src(c), func=AF.Square)),
                 reads=keys(c), writes=[("sq", j)])
            return j

        def ones_pe(c, j):
            S.op("pe", (lambda h, c=c, j=j: h.matmul(bank(PSN), lhsT=ones[:], rhs=sq[:, j, :], start=(c == 0), stop=(c == 7))),
                 reads=[("sq", j), "ones"], writes=bk(PSN))

        def rstd_finish():
            S.op("act", lambda h: h.activation(out=rstd[:], in_=bank(PSN), func=AF.Ln, bias=epst[:, 0:1], scale=1.0 / D),
                 reads=bk(PSN) + ["epst"], writes=["rstd"])
            S.op("act", lambda h: h.activation(out=rstd[:], in_=rstd[:], func=AF.Exp, scale=-0.5), reads=["rstd"], writes=["rstd"])

        def stats_loop(src, keys, ft, nf):
            for c in range(8):
                j = sq_act(src, keys, c)
                rest = fill_pe(ft, share(nf, c))
                ones_pe(c, j)
                for (fa, fd) in rest:
                    fa()
                    fd()
            rstd_finish()

        def hkeys(t):
            return lambda c: [("h", t % NH, c)]

        def hsrc(t):
            return lambda c: Hbuf[:, t % NH, c, :]

        def apply_pre(t, v, dst, dkeys):
            for c in range(8):
                S.op("dve", (lambda h, c=c: h.scalar_tensor_tensor(out=dst(c), in0=Hbuf[:, t % NH, c, :], scalar=gs[:, v, c:c + 1],
                                                                     in1=rstd[:], op0=ALU.mult, op1=ALU.mult)),
                     reads=[("h", t % NH, c), "rstd", "gs"], writes=dkeys(c))

        def pre_only(t, v, dst, dkeys, ft=-1, nf=0):
            stats_loop(hsrc(t), hkeys(t), ft, nf)
            apply_pre(t, v, dst, dkeys)

        def boundary(t, vpost, src, keys, vpre=None, dst=None, dkeys=None, ft=-1, nf1=0, nf2=0, after=None):
            stats_loop(src, keys, ft, nf1)
            pend = []
            for c in range(8):
                i = c % 2
                S.op("dve", (lambda h, c=c, i=i: h.tensor_tensor(out=tmp[:, i, :], in0=src(c), in1=rstd[:], op=ALU.mult)),
                     reads=keys(c) + ["rstd"], writes=[("tmp", i)])
                S.op("dve", (lambda h, c=c, i=i: h.scalar_tensor_tensor(out=Hbuf[:, t % NH, c, :], in0=tmp[:, i, :], scalar=gs[:, vpost, c:c + 1],
                                                                          in1=Hbuf[:, t % NH, c, :], op0=ALU.mult, op1=ALU.add)),
                     reads=[("tmp", i), "gs", ("h", t % NH, c)], writes=[("h", t % NH, c)])
                for fd in pend:
                    fd()
                pend = []
                if after is not None:
                    after(c)
                j = sq_act(hsrc(t), hkeys(t), c) if vpre is not None else None
                rest = fill_pe(ft, share(nf2, c))
                if vpre is not None:
                    ones_pe(c, j)
                for (fa, fd) in rest:
                    fa()
                    pend.append(fd)
            for fd in pend:
                fd()
            if vpre is not None:
                rstd_finish()
                apply_pre(t, vpre, dst, dkeys)

        obank = [0]

        def next_obank():
            b = (4, 5, 7)[obank[0] % 3]
            obank[0] += 1
            return b

        def proj_fm(wname, col0, nchunks, rhs, rkeys, evac):
            wv = wview(wname)
            kc = wv.shape[1]
            for half in range(nchunks // 2):
                wk, wt = wload(wv[:, :, col0 + half * 256:col0 + (half + 1) * 256], kc, 256)
                for j in range(2):
                    ci = half * 2 + j
                    b = next_obank()
                    for k in range(kc):
                        S.op("pe", (lambda h, b=b, k=k, j=j, wt=wt: h.matmul(bank(b), lhsT=wt[:, k, j * 128:(j + 1) * 128], rhs=rhs(k),
                                                                              start=(k == 0), stop=(k == kc - 1))),
                             reads=[wk] + rkeys(k), writes=[*bk(b)])
                    evac(ci, b)

        def ffn_body(t, pfx):
            wg, wu, wd = wview(pfx + "_wg"), wview(pfx + "_wu"), wview(pfx + "_wd")
            for grp in range(NFC // 2):
                kg, tg = wload(wg[:, :, grp * 256:(grp + 1) * 256], 8, 256)
                ku, tu = wload(wu[:, :, grp * 256:(grp + 1) * 256], 8, 256)
                for j in range(2):
                    fc = 2 * grp + j
                    pb = (fc % 2) * 2
                    for (kk, tt, b) in ((kg, tg, pb), (ku, tu, pb + 1)):
                        for k in range(8):
                            S.op("pe", (lambda h, b=b, k=k, j=j, tt=tt: h.matmul(bank(b), lhsT=tt[:, k, j * 128:(j + 1) * 128], rhs=xn[:, k, :],
                                                                                  start=(k == 0), stop=(k == 7))),
                                 reads=[kk, kxn(k)], writes=[*bk(b)])
                    s = fc % 2
                    S.op("act", (lambda h, s=s, pb=pb: h.activation(out=sig[:, s, 0, :], in_=bank(pb), func=AF.Silu)),
                         reads=[*bk(pb)], writes=[("sig", s, 0)])
                    S.op("dve", (lambda h, s=s, pb=pb, fc=fc: h.tensor_tensor(out=hid[:, fc, :], in0=bank(pb + 1), in1=sig[:, s, 0, :], op=ALU.mult)),
                         reads=[*bk(pb + 1), ("sig", s, 0)], writes=[("hid", fc)])
            pieces = ((0, 8), (8, 16), (16, 22))
            for db in range(4):
                wl = [wload(wd[:, f0:f1, db * 256:(db + 1) * 256], f1 - f0, 256) for (f0, f1) in pieces]
                for j in range(2):
                    dc = 2 * db + j
                    b = 4 + dc % 2
                    for fc in range(NFC):
                        pi = 0 if fc < 8 else (1 if fc < 16 else 2)
                        wk, wt = wl[pi]
                        f0 = pieces[pi][0]
                        S.op("pe", (lambda h, b=b, fc=fc, f0=f0, j=j, wt=wt: h.matmul(bank(b), lhsT=wt[:, fc - f0, j * 128:(j + 1) * 128], rhs=hid[:, fc, :],
                                                                                     start=(fc == 0), stop=(fc == NFC - 1))),
                             reads=[wk, ("hid", fc)], writes=[*bk(b)])
                    S.op("act", (lambda h, b=b, dc=dc: h.activation(out=fF[:, dc, :], in_=bank(b), func=AF.Copy)),
                         reads=[*bk(b)], writes=[kF(dc)])

        xdone = [set() for _ in range(NT + 2)]

        def load_x(t, chunks=range(8)):
            if t >= NT:
                return
            s = t % NH
            for c in chunks:
                if c in xdone[t]:
                    continue
                xdone[t].add(c)
                S.op("sp", (lambda h, c=c: h.dma_start(out=Hbuf[:, s, c, :], in_=xTv[:, c, t * T:(t + 1) * T])),
                     writes=[("h", s, c)], dma_key=("x", s, c))

        def store_h(t, chunks=range(8)):
            s = t % NH
            for c in chunks:
                S.op("sp", (lambda h, c=c: h.dma_start(out=outTv[:, c, t * T:(t + 1) * T], in_=Hbuf[:, s, c, :])),
                     reads=[("h", s, c)], writes=[("outd", s, c)], dma_key=("o", s, c))

        def stage_A(t):
            load_x(t)
            m = t % 2
            s3 = t % 3
            pre_only(t, 0, lambda c: xn[:, c, :], lambda c: [kxn(c)], ft=t - 1, nf=FA[0])
            ffn_body(t, "ffn1")
            boundary(t, 1, lambda c: fF[:, c, :], lambda c: [kF(c)], 2, lambda c: xnm[:, m, c, :], lambda c: [("xm", m, c)],
                     ft=t - 1, nf1=FA[1], nf2=FA[2])
            if stop_stage <= 1:
                return
            rhs = lambda k: xnm[:, m, k, :]
            rk = lambda k: [("xm", m, k)]

            def evac_k(ci, b):
                S.op("act", lambda h: h.activation(out=kT[:, s3, ci, :], in_=bank(b), func=AF.Copy),
                     reads=[*bk(b)], writes=[("kT", s3, ci)])

            proj_fm("w_in", 512, 4, rhs, rk, evac_k)

            def evac_xp(ci, b):
                S.op("dve", lambda h: h.tensor_copy(out=xp[:, s3, ci, 8:8 + T], in_=bank(b)),
                     reads=[*bk(b)], writes=[("xpc", s3, ci)])
                S.op("dve", lambda h: h.tensor_copy(out=xp[:, (s3 + 2) % 3, ci, 8 + T:16 + T], in_=bank(b)[:, 0:8]),
                     reads=[*bk(b)], writes=[("xpr", (s3 + 2) % 3, ci)])
                S.op("dve", lambda h: h.tensor_copy(out=xp[:, (s3 + 1) % 3, ci, 0:8], in_=bank(b)[:, T - 8:T]),
                     reads=[*bk(b)], writes=[("xpl", (s3 + 1) % 3, ci)])
                if t == NT - 1:
                    S.op("dve", lambda h: h.memset(xp[:, s3, ci, 8 + T:16 + T], 0.0), writes=[("xpr", s3, ci)])

            proj_fm("w_in", 1536, 4, rhs, rk, evac_xp)
            wv = wview("w_in")
            for piece in range(2):
                wk, wt = wload(wv[:, :, 1024 + piece * 256:1024 + (piece + 1) * 256], 8, 256)
                for blk in range(4):
                    b = next_obank()
                    for k in range(8):
                        S.op("pe", (lambda h, b=b, k=k, blk=blk, wt=wt: h.matmul(bank(b)[:, 0:256], lhsT=xnm[:, m, k, blk * 128:(blk + 1) * 128], rhs=wt[:, k, :],
                                                                                  start=(k == 0), stop=(k == 7))),
                             reads=[wk, ("xm", m, k)], writes=[*bk(b)])
                    S.op("dve", (lambda h, b=b, blk=blk, piece=piece: h.tensor_copy(out=Vr[:, s3, blk, piece * 256:(piece + 1) * 256], in_=bank(b)[:, 0:256])),
                         reads=[*bk(b)], writes=[("V", s3, blk, piece)])

        def xpkeys(s3, g):
            return [("xpc", s3, g), ("xpr", s3, g), ("xpl", s3, g), ("xpl", s3)]

        def emit_qproj(t):
            m = t % 2
            qdone[t] = True

            def evac_q(ci, b):
                S.op("act", lambda h: h.mul(out=qT[:, m, ci, :], in_=bank(b), mul=0.125), reads=[*bk(b)], writes=[("q", m, ci)])

            proj_fm("w_in", 0, 4, lambda k: xnm[:, m, k, :], lambda k: [("xm", m, k)], evac_q)

        ITEMS = [(rp, c) for rp in range(4) for c in range(4)]
        anext = [0] * (NT + 1)
        qdone = [False] * (NT + 1)
        apar = [0]
        apend = [None]

        def geom(t, rp):
            r = 8 * t + 2 * rp
            kb = min(max(r - 4, 0), 54)
            var = {0: 1, 2: 2, 60: 3, 62: 4}.get(r, 0)
            return r, kb, var

        def item_steps(t, i, sbi):
            rp, c = ITEMS[i]
            r, kb, var = geom(t, rp)
            m = t % 2
            st = {}

            def s_pe(_):
                order = [(0, 4)] + [(hh, j) for j in range(4) for hh in (1, 0)] + [(1, 4)]
                for (hh, j) in order:
                    krow = kb + 2 * j
                    ks = (krow // 8) % 3
                    koff = (krow % 8) * 64
                    if j < 4:
                        bnk = sbi * 3 + hh
                        out = bank(bnk)[:, j * 128:(j + 1) * 128]
                    else:
                        bnk = sbi * 3 + 2
                        out = bank(bnk)[:, hh * 128:(hh + 1) * 128]
                    S.op("pe", (lambda h, out=out, hh=hh, ks=ks, koff=koff: h.matmul(out, lhsT=kT[64 * hh:64 * hh + 64, ks, c, koff:koff + 128],
                                                                                   rhs=qT[64 * hh:64 * hh + 64, m, c, rp * 128:(rp + 1) * 128], start=True, stop=True)),
                         reads=[("kT", ks, c), ("q", m, c)], writes=bk(bnk))

            def s_act(_):
                for (bnk, c0, n) in ((sbi * 3, 0, 512), (sbi * 3 + 1, 512, 512), (sbi * 3 + 2, 1024, 256)):
                    S.op("act", (lambda h, bnk=bnk, c0=c0, n=n: h.activation(out=PT[:, sbi, c0:c0 + n], in_=bank(bnk)[:, 0:n], func=AF.Exp)),
                         reads=bk(bnk), writes=[("PT", sbi)])
                if var != 0:
                    wk, wt = wload(biasT_d[var][:, c * 1280:(c + 1) * 1280].rearrange("p (k n) -> p k n", k=1), 1, 1280)
                    S.op("act", lambda h: h.activation(out=wt[:, 0, :], in_=wt[:, 0, :], func=AF.Exp), reads=[wk], writes=[wk])
                    st["e"] = ([wk], wt[:, 0, :])

            def s_dve(_):
                ek, ev = st.get("e", (["biasI"], biasI[:, c * 1280:(c + 1) * 1280]))
                S.op("dve", lambda h: h.tensor_tensor(out=PT[:, sbi, :], in0=PT[:, sbi, :], in1=ev, op=ALU.mult),
                     reads=[("PT", sbi)] + ek, writes=[("PT", sbi)])

            nd = bank(3 * sbi + 2)[:, 256:512]

            ndk = bk(3 * sbi + 2)

            def pv_pe(_):
                for hh in range(2):
                    hd = 2 * c + hh
                    for which in range(2):
                        for j in range(5):
                            krow = kb + 2 * j
                            ks = (krow // 8) % 3
                            blk = (krow % 8) // 2
                            c0 = (hh * 512 + j * 128) if j < 4 else (1024 + hh * 128)
                            pk_ = ("PT", sbi)
                            if which == 0:
                                S.op("pe", (lambda h, hh=hh, hd=hd, ks=ks, blk=blk, c0=c0, j=j: h.matmul(nd[64 * hh:64 * hh + 64, 0:128], lhsT=Vr[:, ks, blk, hd * 64:(hd + 1) * 64],
                                                                                                       rhs=PT[:, sbi, c0:c0 + 128], start=(j == 0), stop=(j == 4))),
                                     reads=[("V", ks, blk, hd // 4), pk_], writes=ndk)
                            else:
                                S.op("pe", (lambda h, hh=hh, c0=c0, j=j: h.matmul(nd[64 * hh:64 * hh + 64, 128:256], lhsT=ones[:, 0:64],
                                                                                   rhs=PT[:, sbi, c0:c0 + 128], start=(j == 0), stop=(j == 4))),
                                     reads=["ones", pk_], writes=ndk)

            def pv_act(_):
                S.op("act", lambda h: h.activation(out=rden[:, sbi, :], in_=nd[:, 128:256], func=AF.Ln), reads=ndk, writes=[("rden", sbi)])
                S.op("act", lambda h: h.activation(out=rden[:, sbi, :], in_=rden[:, sbi, :], func=AF.Exp, scale=-1.0),
                     reads=[("rden", sbi)], writes=[("rden", sbi)])
                S.op("act", lambda h: h.activation(out=numS[:, sbi, :], in_=nd[:, 0:128], func=AF.Copy), reads=ndk, writes=[("numS", sbi)])

            def pv_dve(_):
                S.op("dve", lambda h: h.tensor_tensor(out=attnT[:, m, c, rp * 128:(rp + 1) * 128], in0=numS[:, sbi, :], in1=rden[:, sbi, :], op=ALU.mult),
                     reads=[("numS", sbi), ("rden", sbi)], writes=[("attn", m, c)])

            mk = lambda f, g, k: ((lambda: f(0)), (lambda: g(0)), (lambda: k(0)))
            return [mk(s_pe, s_act, s_dve)], [mk(pv_pe, pv_act, pv_dve)]

        aqueue = [[] for _ in range(NT + 1)]
        apv = [None] * (NT + 1)

        def attn_pop(t, limit):
            if t < 0 or t >= NT or not qdone[t]:
                return None
            q = aqueue[t]
            if not q:
                if anext[t] < limit:
                    i = anext[t]
                    anext[t] += 1
                    sbi = apar[0] % 2
                    apar[0] += 1
                    ss, pv = item_steps(t, i, sbi)
                    q.extend(ss)
                    if apv[t] is not None:
                        q.extend(apv[t])
                    apv[t] = pv
                elif apv[t] is not None and limit >= 16:
                    q.extend(apv[t])
                    apv[t] = None
            if not q:
                return None
            return q.pop(0)

        def fill_pe(t, n):
            rest = []
            if FILL:
                for _ in range(n):
                    stp = attn_pop(t, 8)
                    if stp is None:
                        break
                    stp[0]()
                    rest.append((stp[1], stp[2]))
            return rest

        def attn_run(t):
            while True:
                stp = attn_pop(t, 16)
                if stp is None:
                    break
                stp[0]()
                stp[1]()
                stp[2]()

        def stage_B(t):
            m = t % 2
            s3 = t % 3
            rhs = lambda k: xnm[:, m, k, :]
            rk = lambda k: [("xm", m, k)]
            if t == 0:
                emit_qproj(0)
            if t + 1 < NT:
                emit_qproj(t + 1)
            for g in range(4):
                w = (2, 4, 8, 16)[g]
                L = T + 16
                X = xp[:, s3, g, :]
                S.op("dve", (lambda h, X=X, L=L: h.tensor_tensor(out=ptA[:, 1:L], in0=X[:, 0:L - 1], in1=X[:, 1:L], op=ALU.add)),
                     reads=xpkeys(s3, g), writes=KPTA)
                cur, curk, oth, othk = ptA, KPTA, ptB, KPTB
                lo, hi = 1, L
                sh = 1
                while sh * 2 < w:
                    nlo, nhi = lo + sh, hi - sh
                    S.op("dve", (lambda h, cur=cur, oth=oth, nlo=nlo, nhi=nhi, sh=sh: h.tensor_tensor(out=oth[:, nlo:nhi], in0=cur[:, nlo - sh:nhi - sh],
                                                                                                       in1=cur[:, nlo + sh:nhi + sh], op=ALU.add)),
                         reads=curk, writes=othk)
                    cur, curk, oth, othk = oth, othk, cur, curk
                    lo, hi = nlo, nhi
                    sh *= 2
                assert lo <= 8 and hi >= 8 + T
                S.op("dve", (lambda h, cur=cur, g=g, w=w, X=X: h.scalar_tensor_tensor(out=pooled[:, g, :], in0=cur[:, 8:8 + T], scalar=1.0 / w, in1=X[:, 8:8 + T],
                                                                                 op0=ALU.mult, op1=ALU.subtract)),
                     reads=curk + xpkeys(s3, g), writes=[("F", 4 + g // 2)])
                for (cond, off, io) in ((t == 0, 0, 0), (t == NT - 1, T - 8, 8)):
                    if cond:
                        S.op("dve", (lambda h, cur=cur, g=g, off=off, io=io: h.tensor_tensor(out=etmp[:], in0=cur[:, 8 + off:16 + off], in1=invc[:, g, io:io + 8], op=ALU.mult)),
                             reads=curk + ["invc"], writes=["etmp"])
                        S.op("dve", (lambda h, g=g, off=off, X=X: h.tensor_tensor(out=pooled[:, g, off:off + 8], in0=etmp[:], in1=X[:, 8 + off:16 + off], op=ALU.subtract)),
                             reads=["etmp"] + xpkeys(s3, g), writes=[("F", 4 + g // 2)])
            pwk, pwt = wload(Wd["pool_w"].rearrange("g c d -> c g d"), 4, 128)
            for g in range(4):
                b = next_obank()
                S.op("pe", (lambda h, b=b, g=g: h.matmul(bank(b), lhsT=pwt[:, g, :], rhs=pooled[:, g, :], start=True, stop=True)),
                     reads=[pwk, ("F", 4 + g // 2)], writes=[*bk(b)])
                S.op("dve", (lambda h, b=b, g=g: h.tensor_scalar(out=yp2[:, g, :], in0=bank(b), scalar1=pscale[:, g:g + 1], scalar2=None, op0=ALU.mult)),
                     reads=[*bk(b), "pscale"], writes=[("F", 6 + g // 2)])

            attn_run(t)


            wba, wbp, win = wview("w_br_attn"), wview("w_br_pool"), wview("w_in")
            for db in range(4):
                ka, ta = wload(wba[:, :, db * 256:(db + 1) * 256], 4, 256)
                kp, tp = wload(wbp[:, :, db * 256:(db + 1) * 256], 4, 256)
                kga, tga = wload(win[:, :, 2048 + db * 256:2048 + (db + 1) * 256], 8, 256)
                kgp, tgp = wload(win[:, :, 3072 + db * 256:3072 + (db + 1) * 256], 8, 256)
                for j in range(2):
                    dc = 2 * db + j
                    s = dc % 2
                    b0 = s * 4
                    for k in range(4):
                        S.op("pe", (lambda h, k=k, j=j, b0=b0, ta=ta: h.matmul(bank(b0), lhsT=ta[:, k, j * 128:(j + 1) * 128], rhs=attnT[:, m, k, :], start=(k == 0), stop=(k == 3))),
                             reads=[ka, ("attn", m, k)], writes=pk(b0))
                    for k in range(4):
                        S.op("pe", (lambda h, k=k, j=j, b0=b0, tp=tp: h.matmul(bank(b0 + 1), lhsT=tp[:, k, j * 128:(j + 1) * 128], rhs=yp2[:, k, :], start=(k == 0), stop=(k == 3))),
                             reads=[kp, ("F", 6 + k // 2)], writes=pk(b0 + 1))
                    for (kk, tt, bo) in ((kga, tga, 2), (kgp, tgp, 3)):
                        for k in range(8):
                            S.op("pe", (lambda h, k=k, j=j, b0=b0, tt=tt, bo=bo: h.matmul(bank(b0 + bo), lhsT=tt[:, k, j * 128:(j + 1) * 128], rhs=xnm[:, m, k, :],
                                                                                       start=(k == 0), stop=(k == 7))),
                                 reads=[kk, ("xm", m, k)], writes=pk(b0 + bo))
                    for q in range(2):
                        S.op("act", (lambda h, q=q, s=s, b0=b0: h.activation(out=sig[:, s, q, :], in_=bank(b0 + 2 + q), func=AF.Sigmoid)),
                             reads=pk(b0 + 2 + q), writes=[("sig", s, q)])
                        S.op("dve", (lambda h, q=q, s=s, b0=b0: h.tensor_tensor(out=sig[:, s, q, :], in0=bank(b0 + q), in1=sig[:, s, q, :], op=ALU.mult)),
                             reads=pk(b0 + q) + [("sig", s, q)], writes=[("sig", s, q)])
                    S.op("dve", (lambda h, s=s, dc=dc: h.tensor_tensor(out=xn[:, dc, :], in0=sig[:, s, 0, :], in1=sig[:, s, 1, :], op=ALU.add)),
                         reads=[("sig", s, 0), ("sig", s, 1)], writes=[kxn(dc)])

            def evac_m(ci, b):
                S.op("act", lambda h: h.activation(out=fH[:, ci, :], in_=bank(b), func=AF.Copy), reads=[*bk(b)], writes=kfH(ci))

            if debug and t == int(os.environ.get("K_DBG_T", "0")):
                def dump(nm, src, keys):
                    S.op("sp", lambda h: h.dma_start(out=dbg[nm], in_=src), reads=keys, writes=[("dbg", nm)], dma_key=("dbg", nm))
                dump("attn", attnT[:, m], [("attn", m, c) for c in range(4)])
                dump("pooled", pooled, [("F", 4), ("F", 5)])
                dump("yp2", yp2, [("F", 6), ("F", 7)])
                dump("merged", xn, [("F", c) for c in range(4)])
                dump("q", qT[:, m], [("q", m, c) for c in range(4)])
                dump("k", kT[:, 0, :, :], [("kT", 0, c) for c in range(4)])
                dump("v", Vr[:, 0, :, :], [("V", 0, b, p_) for b in range(4) for p_ in range(2)])
            proj_fm("w_out", 0, 8, lambda k: xn[:, k, :], lambda k: [kxn(k)], evac_m)
            boundary(t, 3, lambda c: fH[:, c, :], kfH, 4, lambda c: xn[:, c, :], lambda c: [kxn(c)], ft=t + 1, nf1=FB[0], nf2=FB[1])
            ffn_body(t, "ffn2")
            boundary(t, 5, lambda c: fF[:, c, :], lambda c: [kF(c)], 6, lambda c: xn[:, c, :], lambda c: [kxn(c)], ft=t + 1, nf1=FB[2], nf2=FB[3])
            S.op("pool", lambda h: h.dma_start(out=pTb[:], in_=pTv[:, :, t * T:(t + 1) * T]), writes=["pTb"], dma_key="pTb")
            wp, wgt = wview("ple_wp"), wview("ple_wg")
            for db in range(4):
                kp_, tp_ = wload(wp[:, :, db * 256:(db + 1) * 256], 2, 256)
                kg_, tg_ = wload(wgt[:, :, db * 256:(db + 1) * 256], 8, 256)
                for j in range(2):
                    dc = 2 * db + j
                    s = dc % 2
                    b0 = s * 2
                    for k in range(2):
                        S.op("pe", (lambda h, k=k, j=j, b0=b0, tp_=tp_: h.matmul(bank(b0), lhsT=tp_[:, k, j * 128:(j + 1) * 128], rhs=pTb[:, k, :], start=(k == 0), stop=(k == 1))),
                             reads=[kp_, "pTb"], writes=[*bk(b0)])
                    for k in range(8):
                        S.op("pe", (lambda h, k=k, j=j, b0=b0, tg_=tg_: h.matmul(bank(b0 + 1), lhsT=tg_[:, k, j * 128:(j + 1) * 128], rhs=xn[:, k, :], start=(k == 0), stop=(k == 7))),
                             reads=[kg_, kxn(k)], writes=[*bk(b0 + 1)])
                    S.op("act", (lambda h, s=s, b0=b0: h.activation(out=sig[:, s, 0, :], in_=bank(b0 + 1), func=AF.Sigmoid)),
                         reads=[*bk(b0 + 1)], writes=[("sig", s, 0)])
                    S.op("dve", (lambda h, s=s, b0=b0, dc=dc: h.tensor_tensor(out=fH[:, dc, :], in0=bank(b0), in1=sig[:, s, 0, :], op=ALU.mult)),
                         reads=[*bk(b0), ("sig", s, 0)], writes=kfH(dc))
            def chase(c):
                store_h(t, [c])
                if NH == 2 and c >= 2:
                    load_x(t + 2, [c - 2])

            boundary(t, 7, lambda c: fH[:, c, :], kfH, ft=t + 1, nf1=FB[4], nf2=FB[5], after=chase)
            if NH == 2:
                load_x(t + 2, [6, 7])

        if stop_stage <= 1:
            for t in range(nt_run):
                stage_A(t)
                store_h(t)
        else:
            stage_A(0)
            for t in range(nt_run):
                if t + 1 < NT:
                    stage_A(t + 1)
                stage_B(t)
        S.op("sp", None, reads=[("outd", s, c) for s in range(NH) for c in range(8)] + [("dbg", nm) for nm in dbg])
        S.emit(nc, st)
    return nc


def _bias_table(rpb):
    out = np.full((5, 128, 8, 5, 128), NEG, dtype=np.float32)
    kl, kc = np.divmod(np.arange(128), 64)
    ql, qc = np.divmod(np.arange(128), 64)
    cs = np.clip(qc - 8, 0, 48)
    for var, r in enumerate((4, 0, 2, 60, 62)):
        kb = min(max(r - 4, 0), 54)
        for j in range(5):
            key_row = kb + 2 * j + kl
            q_row = r + ql
            rs = np.clip(q_row - 4, 0, 56)
            vr = (key_row[:, None] >= rs[None, :]) & (key_row[:, None] < rs[None, :] + 8)
            vc = (kc[:, None] >= cs[None, :]) & (kc[:, None] < cs[None, :] + 16)
            dr = np.clip(key_row[:, None] - q_row[None, :] + 7, 0, 14)
            dcc = np.clip(kc[:, None] - qc[None, :] + 15, 0, 30)
            valid = vr & vc
            for h in range(8):
                out[var, :, h, j, :] = np.where(valid, rpb[h][dr, dcc], np.float32(NEG))
    segs = []
    for c in range(4):
        segs.append(np.concatenate([out[:, :, 2 * c, 0:4, :].reshape(5, 128, 512), out[:, :, 2 * c + 1, 0:4, :].reshape(5, 128, 512),
                                    out[:, :, 2 * c, 4, :], out[:, :, 2 * c + 1, 4, :]], axis=2))
    return np.ascontiguousarray(np.concatenate(segs, axis=2))


def _chunked(v):
    return np.ascontiguousarray(v.reshape(-1, 128).T)


def prepare_inputs(inputs):
    g = lambda k: np.asarray(inputs[k], dtype=np.float32)
    shared = {
        "ffn1_wg": g("ffn1_w_gate")[0], "ffn1_wu": g("ffn1_w_up")[0], "ffn1_wd": g("ffn1_w_down")[0],
        "w_in": g("w_in")[0], "pool_w": g("pool_w")[0], "w_br_attn": g("w_br_attn")[0], "w_br_pool": g("w_br_pool")[0],
        "w_out": g("w_out")[0],
        "ffn2_wg": g("ffn2_w_gate")[0], "ffn2_wu": g("ffn2_w_up")[0], "ffn2_wd": g("ffn2_w_down")[0],
        "ple_wp": g("ple_w_proj")[0], "ple_wg": g("ple_w_gate")[0],
    }
    names = ["ffn1_pre_g", "ffn1_post_g", "mix_pre_g", "mix_post_g", "ffn2_pre_g", "ffn2_post_g", "ple_pre_g", "ple_post_g"]
    shared["gains"] = np.ascontiguousarray(np.stack([_chunked(g(n)[0]) for n in names], axis=1))
    shared["pscale"] = _chunked(g("pool_scale")[0])
    shared["biasT"] = _bias_table(g("rpb")[0])
    invc = np.zeros((128, 4, 16), np.float32)
    for gi, w in enumerate((2, 4, 8, 16)):
        half = w // 2
        for i in range(8):
            tl = i
            tr = SEQ - 8 + i
            invc[:, gi, i] = 1.0 / (min(tl + half, SEQ) - max(tl - half, 0))
            invc[:, gi, 8 + i] = 1.0 / (min(tr + half, SEQ) - max(tr - half, 0))
    shared["invc"] = invc
    shared["ident"] = np.eye(128, dtype=np.float32)
    shared = {k: np.ascontiguousarray(v) for k, v in shared.items()}
    x = g("x")
    p = g("p")[0]
    maps = []
    for b in range(x.shape[0]):
        m = dict(shared)
        m["xT"] = np.ascontiguousarray(x[b].T)
        m["pT"] = np.ascontiguousarray(p[b].T)
        maps.append(m)
    return maps


_NC_CACHE = {}


def kernel(**inputs):
    maps = prepare_inputs(inputs)
    if "nc" not in _NC_CACHE:
        _NC_CACHE["nc"] = build_program()
    nc = _NC_CACHE["nc"]
    res = run_bass_kernel_spmd(nc, maps, core_ids=list(range(8)))
    out = np.stack([np.ascontiguousarray(r["outT"].T) for r in res.results], axis=0)
    return out.astype(np.float32)
```

```python
import os
import numpy as np
from contextlib import ExitStack
import concourse.bass as bass
import concourse.mybir as mybir
from concourse.bass_utils import run_bass_kernel_spmd

F32 = mybir.dt.float32
BF16 = mybir.dt.bfloat16
AF = mybir.ActivationFunctionType
ALU = mybir.AluOpType

D = 1024
SEQ = 4096
T = 512
NT = SEQ // T
DFF = 2816
NFC = DFF // 128
NEG = -30000.0
NH = 2
NW = 7
FILL = os.environ.get("K_FILL", "1") == "1"
FB = tuple(int(v) for v in os.environ.get("K_FB", "0,4,0,4,0,4").split(","))
FA = tuple(int(v) for v in os.environ.get("K_FA", "0,0,4").split(","))
WSLOT = 2048
PSN = 6


class Op:
    __slots__ = ("eng", "fn", "deps", "idx", "inc", "semval", "dma_key", "dma_cnt")

    def __init__(self, eng, fn, dma_key=None):
        self.eng = eng
        self.fn = fn
        self.deps = []
        self.idx = -1
        self.inc = False
        self.semval = 0
        self.dma_key = dma_key
        self.dma_cnt = 0


ENGS = ("pe", "act", "dve", "pool", "sp")


class Sched:
    def __init__(self):
        self.ops = {e: [] for e in ENGS}
        self.last_w = {}
        self.readers = {}
        self.dma_count = {}
        self.all_ops = []

    def op(self, eng, fn, reads=(), writes=(), dma_key=None):
        o = Op(eng, fn, dma_key)
        deps = {}

        def add(d):
            if d is None:
                return
            if d.dma_key is None and d.eng == "pe" and eng == "pe" and dma_key is None:
                return
            k = ("dma", d.dma_key) if d.dma_key is not None else ("eng", d.eng)
            cur = deps.get(k)
            if cur is None or (d.dma_key is None and d.idx > cur.idx) or (d.dma_key is not None and d.dma_cnt > cur.dma_cnt):
                deps[k] = d

        for r in reads:
            add(self.last_w.get(r))
        for w in writes:
            add(self.last_w.get(w))
            for rd in self.readers.get(w, ()):
                add(rd)
        o.deps = list(deps.values())
        o.idx = len(self.ops[eng])
        self.ops[eng].append(o)
        self.all_ops.append(o)
        if dma_key is not None:
            self.dma_count[dma_key] = self.dma_count.get(dma_key, 0) + 16
            o.dma_cnt = self.dma_count[dma_key]
        for r in reads:
            lst = self.readers.setdefault(r, [])
            if dma_key is None:
                lst[:] = [x for x in lst if not (x.dma_key is None and x.eng == eng)]
            lst.append(o)
        for w in writes:
            self.last_w[w] = o
            self.readers[w] = []
        return o

    def emit(self, nc, stack):
        for o in self.all_ops:
            for d in o.deps:
                if d.dma_key is None:
                    d.inc = True
        esem = {}
        for e in ENGS:
            esem[e] = stack.enter_context(nc.semaphore("s_" + e))
            c = 0
            for o in self.ops[e]:
                if o.dma_key is None and o.inc:
                    c += 1
                    o.semval = c
        dsem = {}
        for k in self.dma_count:
            dsem[k] = stack.enter_context(nc.semaphore("d%d" % len(dsem)))
        block = stack.enter_context(nc.Block())

        def run(e, h):
            waited = {}
            for o in self.ops[e]:
                for d in o.deps:
                    if d.dma_key is not None:
                        s, v = dsem[d.dma_key], d.dma_cnt
                    else:
                        s, v = esem[d.eng], d.semval
                    if waited.get(s.num, 0) >= v:
                        continue
                    waited[s.num] = v
                    h.wait_ge(s, v)
                if o.fn is None:
                    continue
                ins = o.fn(h)
                if o.dma_key is not None:
                    ins.then_inc(dsem[o.dma_key], 16)
                elif o.inc:
                    ins.then_inc(esem[e], 1)

        @block.tensor
        def _(h):
            run("pe", h)

        @block.scalar
        def _(h):
            run("act", h)

        @block.vector
        def _(h):
            run("dve", h)

        @block.gpsimd
        def _(h):
            run("pool", h)

        @block.sync
        def _(h):
            run("sp", h)


WSHAPES = {
    "ffn1_wg": [D, DFF], "ffn1_wu": [D, DFF], "ffn1_wd": [DFF, D],
    "w_in": [D, 4096], "pool_w": [4, 128, 128], "w_br_attn": [512, D], "w_br_pool": [512, D], "w_out": [D, D],
    "ffn2_wg": [D, DFF], "ffn2_wu": [D, DFF], "ffn2_wd": [DFF, D],
    "ple_wp": [256, D], "ple_wg": [D, D],
}


def build_program(nt_run=NT, stop_stage=99, debug=False):
    nc = bass.Bass("TRN2", target_bir_lowering=False)

    def din(name, shape):
        return nc.dram_tensor(name, shape, F32, kind="ExternalInput").ap()

    xT = din("xT", [D, SEQ])
    pT = din("pT", [256, SEQ])
    Wd = {k: din(k, s) for k, s in WSHAPES.items()}
    gains_d = din("gains", [128, 8, 8])
    pscale_d = din("pscale", [128, 4])
    biasT_d = din("biasT", [5, 128, 5120])
    invc_d = din("invc", [128, 4, 16])
    ident_d = din("ident", [128, 128])
    outT = nc.dram_tensor("outT", [D, SEQ], F32, kind="ExternalOutput").ap()

    dbg = {}
    if debug:
        for nm, shp in (("attn", [128, 4, T]), ("pooled", [128, 4, T]), ("yp2", [128, 4, T]), ("merged", [128, 8, T]), ("q", [128, 4, T]), ("k", [128, 4, T]), ("v", [128, 4, 512])):
            dbg[nm] = nc.dram_tensor("dbg_" + nm, shp, BF16, kind="ExternalOutput").ap()
    xTv = xT.rearrange("(c p) s -> p c s", p=128)
    outTv = outT.rearrange("(c p) s -> p c s", p=128)
    pTv = pT.rearrange("(c p) s -> p c s", p=128)

    def wview(name):
        return Wd[name].rearrange("(k p) n -> p k n", p=128)

    S = Sched()
    with ExitStack() as st:
        def sb(name, shape, dt):
            return st.enter_context(nc.sbuf_tensor(name, shape, dt))

        Hbuf = sb("Hbuf", [128, NH, 8, T], F32)
        bufF = sb("bufF", [128, 8 * T], F32)
        sq = sb("sq", [128, 4, T], BF16)
        xnm = sb("xnm", [128, 2, 8, T], BF16)
        hidA = sb("hidA", [128, NFC * T // 2], F32)
        kT = sb("kT", [128, 3, 4, T], BF16)
        Vr = sb("Vr", [128, 3, 4, 512], BF16)
        qT = sb("qT", [128, 2, 4, T], BF16)
        xp = sb("xp", [128, 3, 4, T + 16], BF16)
        PT = sb("PT", [128, 2, 1280], BF16)
        attnT = sb("attnT", [128, 2, 4, T], BF16)
        rden = sb("rden", [128, 2, 128], F32)
        numS = sb("numS", [128, 2, 128], F32)
        biasI = sb("biasI", [128, 5120], BF16)
        sig = sb("sig", [128, 2, 2, T], F32)
        rstd = sb("rstd", [128, T], F32)
        tmp = sb("tmp", [128, 2, T], F32)
        pTb = sb("pTb", [128, 2, T], BF16)
        wring = [sb("w%d" % i, [128, WSLOT], BF16) for i in range(NW)]
        ident = sb("identb", [128, 128], BF16)
        ones = sb("onesb", [128, 128], BF16)
        gs = sb("gs", [128, 8, 8], F32)
        pscale = sb("pscale_sb", [128, 4], F32)
        epst = sb("epst", [128, 1], F32)
        invc = sb("invc_sb", [128, 4, 16], F32)
        etmp = sb("etmp", [128, 8], F32)
        ps = st.enter_context(nc.psum_tensor("ps", [128, 8 * 512], F32))

        def bank(b):
            return ps[:, b * 512:(b + 1) * 512]

        def bk(b):
            return [("ps", b, 0), ("ps", b, 1)]

        def pk(b):
            return bk(b)

        fF = bufF[:].rearrange("p (c t) -> p c t", c=8)
        bF16 = bufF[:].bitcast(BF16)
        xn = bF16[:, 0:8 * T].rearrange("p (c t) -> p c t", c=8)
        pooled = bF16[:, 8 * T:12 * T].rearrange("p (c t) -> p c t", c=4)
        yp2 = bF16[:, 12 * T:16 * T].rearrange("p (c t) -> p c t", c=4)
        hid = hidA[:].bitcast(BF16).rearrange("p (c t) -> p c t", c=NFC)
        fH = hidA[:, 0:8 * T].rearrange("p (c t) -> p c t", c=8)
        ptA = hidA[:, 8 * T:8 * T + 544]
        ptB = hidA[:, 8 * T + 768:8 * T + 768 + 544]

        def kF(c):
            return ("F", c)

        def kxn(c):
            return ("F", c // 2)

        def kfH(c):
            return [("hid", 2 * c), ("hid", 2 * c + 1)]

        KPTA = [("hid", 16), ("hid", 17), ("hid", 18)]
        KPTB = [("hid", 19), ("hid", 20), ("hid", 21)]

        S.op("sp", lambda h: h.dma_start(out=gs[:], in_=gains_d), writes=["gs"], dma_key="c_gs")
        S.op("sp", lambda h: h.dma_start(out=pscale[:], in_=pscale_d), writes=["pscale"], dma_key="c_ps")
        S.op("sp", lambda h: h.dma_start(out=invc[:], in_=invc_d), writes=["invc"], dma_key="c_ic")
        S.op("pool", lambda h: h.dma_start(out=ident[:], in_=ident_d), writes=["ident"], dma_key="c_id")
        S.op("pool", lambda h: h.dma_start(out=biasI[:], in_=biasT_d[0]), writes=["biasI"], dma_key="c_bi")
        for q4 in range(4):
            S.op("act", (lambda h, q4=q4: h.activation(out=biasI[:, q4 * 1280:(q4 + 1) * 1280], in_=biasI[:, q4 * 1280:(q4 + 1) * 1280], func=AF.Exp)),
                 reads=["biasI"], writes=["biasI"])
        S.op("dve", lambda h: h.memset(ones[:], 1.0), writes=["ones"])
        S.op("dve", lambda h: h.memset(epst[:], 1e-6), writes=["epst"])
        S.op("dve", lambda h: h.memset(xp[:, 0, :, 0:8], 0.0), writes=[("xpl", 0)])
        for v in (1, 5):
            S.op("dve", (lambda h, v=v: h.tensor_scalar(out=gs[:, v, :], in0=gs[:, v, :], scalar1=0.5, scalar2=None, op0=ALU.mult)),
                 reads=["gs"], writes=["gs"])

        wctr = [0]

        def wload(src, kc, ncols):
            i = wctr[0] % NW
            wctr[0] += 1
            dst = wring[i][:, 0:kc * ncols].rearrange("p (k n) -> p k n", k=kc)
            S.op("pool", lambda h: h.dma_start(out=dst, in_=src), writes=[("w", i)], dma_key=("w", i))
            return ("w", i), dst

        sqctr = [0]

        def share(n, c):
            if n <= 0:
                return 0
            slots = [0 if i < 2 else min(7, (i * 8) // n) for i in range(n)]
            return slots.count(c)

        def sq_act(src, keys, c):
            j = sqctr[0] % 4
            sqctr[0] += 1
            S.op("act", (lambda h, c=c, j=j: h.activation(out=sq[:, j, :], in_=src(c), func=AF.Square)),
                 reads=keys(c), writes=[("sq", j)])
            return j

        def ones_pe(c, j):
            S.op("pe", (lambda h, c=c, j=j: h.matmul(bank(PSN), lhsT=ones[:], rhs=sq[:, j, :], start=(c == 0), stop=(c == 7))),
                 reads=[("sq", j), "ones"], writes=bk(PSN))

        def rstd_finish():
            S.op("act", lambda h: h.activation(out=rstd[:], in_=bank(PSN), func=AF.Ln, bias=epst[:, 0:1], scale=1.0 / D),
                 reads=bk(PSN) + ["epst"], writes=["rstd"])
            S.op("act", lambda h: h.activation(out=rstd[:], in_=rstd[:], func=AF.Exp, scale=-0.5), reads=["rstd"], writes=["rstd"])

        def stats_loop(src, keys, ft, nf):
            for c in range(8):
                j = sq_act(src, keys, c)
                rest = fill_pe(ft, share(nf, c))
                ones_pe(c, j)
                for (fa, fd) in rest:
                    fa()
                    fd()
            rstd_finish()

        class TrailStats:
            def __init__(self, src, keys):
                self.src, self.keys, self.pend = src, keys, None

            def before_evac(self):
                if self.pend is not None:
                    ones_pe(*self.pend)
                    self.pend = None

            def after_evac(self, c):
                self.pend = (c, sq_act(self.src, self.keys, c))

            def finish(self):
                self.before_evac()

        def hkeys(t):
            return lambda c: [("h", t % NH, c)]

        def hsrc(t):
            return lambda c: Hbuf[:, t % NH, c, :]

        def apply_pre(t, v, dst, dkeys):
            for c in range(8):
                S.op("dve", (lambda h, c=c: h.scalar_tensor_tensor(out=dst(c), in0=Hbuf[:, t % NH, c, :], scalar=gs[:, v, c:c + 1],
                                                                     in1=rstd[:], op0=ALU.mult, op1=ALU.mult)),
                     reads=[("h", t % NH, c), "rstd", "gs"], writes=dkeys(c))

        def pre_only(t, v, dst, dkeys, ft=-1, nf=0):
            stats_loop(hsrc(t), hkeys(t), ft, nf)
            apply_pre(t, v, dst, dkeys)

        def boundary(t, vpost, src, keys, vpre=None, dst=None, dkeys=None, ft=-1, nf1=0, nf2=0, after=None, pre_stats=False):
            if pre_stats:
                rstd_finish()
            else:
                stats_loop(src, keys, ft, nf1)
            pend = []
            for c in range(8):
                i = c % 2
                S.op("dve", (lambda h, c=c, i=i: h.tensor_tensor(out=tmp[:, i, :], in0=src(c), in1=rstd[:], op=ALU.mult)),
                     reads=keys(c) + ["rstd"], writes=[("tmp", i)])
                S.op("dve", (lambda h, c=c, i=i: h.scalar_tensor_tensor(out=Hbuf[:, t % NH, c, :], in0=tmp[:, i, :], scalar=gs[:, vpost, c:c + 1],
                                                                          in1=Hbuf[:, t % NH, c, :], op0=ALU.mult, op1=ALU.add)),
                     reads=[("tmp", i), "gs", ("h", t % NH, c)], writes=[("h", t % NH, c)])
                for fd in pend:
                    fd()
                pend = []
                if after is not None:
                    after(c)
                j = sq_act(hsrc(t), hkeys(t), c) if vpre is not None else None
                rest = fill_pe(ft, share(nf2, c))
                if vpre is not None:
                    ones_pe(c, j)
                for (fa, fd) in rest:
                    fa()
                    pend.append(fd)
            for fd in pend:
                fd()
            if vpre is not None:
                rstd_finish()
                apply_pre(t, vpre, dst, dkeys)

        obank = [0]

        def next_obank():
            b = (4, 5, 7)[obank[0] % 3]
            obank[0] += 1
            return b

        def proj_fm(wname, col0, nchunks, rhs, rkeys, evac):
            wv = wview(wname)
            kc = wv.shape[1]
            for half in range(nchunks // 2):
                wk, wt = wload(wv[:, :, col0 + half * 256:col0 + (half + 1) * 256], kc, 256)
                for j in range(2):
                    ci = half * 2 + j
                    b = next_obank()
                    for k in range(kc):
                        S.op("pe", (lambda h, b=b, k=k, j=j, wt=wt: h.matmul(bank(b), lhsT=wt[:, k, j * 128:(j + 1) * 128], rhs=rhs(k),
                                                                              start=(k == 0), stop=(k == kc - 1))),
                             reads=[wk] + rkeys(k), writes=[*bk(b)])
                    evac(ci, b)

        def ffn_body(t, pfx):
            wg, wu, wd = wview(pfx + "_wg"), wview(pfx + "_wu"), wview(pfx + "_wd")
            for grp in range(NFC // 2):
                kg, tg = wload(wg[:, :, grp * 256:(grp + 1) * 256], 8, 256)
                ku, tu = wload(wu[:, :, grp * 256:(grp + 1) * 256], 8, 256)
                for j in range(2):
                    fc = 2 * grp + j
                    pb = (fc % 2) * 2
                    for (kk, tt, b) in ((kg, tg, pb), (ku, tu, pb + 1)):
                        for k in range(8):
                            S.op("pe", (lambda h, b=b, k=k, j=j, tt=tt: h.matmul(bank(b), lhsT=tt[:, k, j * 128:(j + 1) * 128], rhs=xn[:, k, :],
                                                                                  start=(k == 0), stop=(k == 7))),
                                 reads=[kk, kxn(k)], writes=[*bk(b)])
                    s = fc % 2
                    S.op("act", (lambda h, s=s, pb=pb: h.activation(out=sig[:, s, 0, :], in_=bank(pb), func=AF.Silu)),
                         reads=[*bk(pb)], writes=[("sig", s, 0)])
                    S.op("dve", (lambda h, s=s, pb=pb, fc=fc: h.tensor_tensor(out=hid[:, fc, :], in0=bank(pb + 1), in1=sig[:, s, 0, :], op=ALU.mult)),
                         reads=[*bk(pb + 1), ("sig", s, 0)], writes=[("hid", fc)])
            pieces = ((0, 8), (8, 16), (16, 22))
            ts = TrailStats(lambda c: fF[:, c, :], lambda c: [kF(c)])
            for db in range(4):
                wl = [wload(wd[:, f0:f1, db * 256:(db + 1) * 256], f1 - f0, 256) for (f0, f1) in pieces]
                for j in range(2):
                    dc = 2 * db + j
                    b = 4 + dc % 2
                    for fc in range(NFC):
                        pi = 0 if fc < 8 else (1 if fc < 16 else 2)
                        wk, wt = wl[pi]
                        f0 = pieces[pi][0]
                        S.op("pe", (lambda h, b=b, fc=fc, f0=f0, j=j, wt=wt: h.matmul(bank(b), lhsT=wt[:, fc - f0, j * 128:(j + 1) * 128], rhs=hid[:, fc, :],
                                                                                     start=(fc == 0), stop=(fc == NFC - 1))),
                             reads=[wk, ("hid", fc)], writes=[*bk(b)])
                    ts.before_evac()
                    S.op("act", (lambda h, b=b, dc=dc: h.activation(out=fF[:, dc, :], in_=bank(b), func=AF.Copy)),
                         reads=[*bk(b)], writes=[kF(dc)])
                    ts.after_evac(dc)
            ts.finish()

        xdone = [set() for _ in range(NT + 2)]

        def load_x(t, chunks=range(8)):
            if t >= NT:
                return
            s = t % NH
            for c in chunks:
                if c in xdone[t]:
                    continue
                xdone[t].add(c)
                S.op("sp", (lambda h, c=c: h.dma_start(out=Hbuf[:, s, c, :], in_=xTv[:, c, t * T:(t + 1) * T])),
                     writes=[("h", s, c)], dma_key=("x", s, c))

        def store_h(t, chunks=range(8)):
            s = t % NH
            for c in chunks:
                S.op("sp", (lambda h, c=c: h.dma_start(out=outTv[:, c, t * T:(t + 1) * T], in_=Hbuf[:, s, c, :])),
                     reads=[("h", s, c)], writes=[("outd", s, c)], dma_key=("o", s, c))

        def stage_A(t):
            load_x(t)
            m = t % 2
            s3 = t % 3
            pre_only(t, 0, lambda c: xn[:, c, :], lambda c: [kxn(c)], ft=t - 1, nf=FA[0])
            ffn_body(t, "ffn1")
            boundary(t, 1, lambda c: fF[:, c, :], lambda c: [kF(c)], 2, lambda c: xnm[:, m, c, :], lambda c: [("xm", m, c)],
                     ft=t - 1, nf1=FA[1], nf2=FA[2], pre_stats=True)
            if stop_stage <= 1:
                return
            rhs = lambda k: xnm[:, m, k, :]
            rk = lambda k: [("xm", m, k)]

            def evac_k(ci, b):
                S.op("act", lambda h: h.activation(out=kT[:, s3, ci, :], in_=bank(b), func=AF.Copy),
                     reads=[*bk(b)], writes=[("kT", s3, ci)])

            proj_fm("w_in", 512, 4, rhs, rk, evac_k)

            def evac_xp(ci, b):
                S.op("dve", lambda h: h.tensor_copy(out=xp[:, s3, ci, 8:8 + T], in_=bank(b)),
                     reads=[*bk(b)], writes=[("xpc", s3, ci)])
                S.op("dve", lambda h: h.tensor_copy(out=xp[:, (s3 + 2) % 3, ci, 8 + T:16 + T], in_=bank(b)[:, 0:8]),
                     reads=[*bk(b)], writes=[("xpr", (s3 + 2) % 3, ci)])
                S.op("dve", lambda h: h.tensor_copy(out=xp[:, (s3 + 1) % 3, ci, 0:8], in_=bank(b)[:, T - 8:T]),
                     reads=[*bk(b)], writes=[("xpl", (s3 + 1) % 3, ci)])
                if t == NT - 1:
                    S.op("dve", lambda h: h.memset(xp[:, s3, ci, 8 + T:16 + T], 0.0), writes=[("xpr", s3, ci)])

            proj_fm("w_in", 1536, 4, rhs, rk, evac_xp)
            wv = wview("w_in")
            for piece in range(2):
                wk, wt = wload(wv[:, :, 1024 + piece * 256:1024 + (piece + 1) * 256], 8, 256)
                for blk in range(4):
                    b = next_obank()
                    for k in range(8):
                        S.op("pe", (lambda h, b=b, k=k, blk=blk, wt=wt: h.matmul(bank(b)[:, 0:256], lhsT=xnm[:, m, k, blk * 128:(blk + 1) * 128], rhs=wt[:, k, :],
                                                                                  start=(k == 0), stop=(k == 7))),
                             reads=[wk, ("xm", m, k)], writes=[*bk(b)])
                    S.op("dve", (lambda h, b=b, blk=blk, piece=piece: h.tensor_copy(out=Vr[:, s3, blk, piece * 256:(piece + 1) * 256], in_=bank(b)[:, 0:256])),
                         reads=[*bk(b)], writes=[("V", s3, blk, piece)])

        def xpkeys(s3, g):
            return [("xpc", s3, g), ("xpr", s3, g), ("xpl", s3, g), ("xpl", s3)]

        def emit_qproj(t):
            m = t % 2
            qdone[t] = True

            def evac_q(ci, b):
                S.op("act", lambda h: h.mul(out=qT[:, m, ci, :], in_=bank(b), mul=0.125), reads=[*bk(b)], writes=[("q", m, ci)])

            proj_fm("w_in", 0, 4, lambda k: xnm[:, m, k, :], lambda k: [("xm", m, k)], evac_q)

        ITEMS = [(rp, c) for rp in range(4) for c in range(4)]
        anext = [0] * (NT + 1)
        qdone = [False] * (NT + 1)
        apar = [0]
        apend = [None]

        def geom(t, rp):
            r = 8 * t + 2 * rp
            kb = min(max(r - 4, 0), 54)
            var = {0: 1, 2: 2, 60: 3, 62: 4}.get(r, 0)
            return r, kb, var

        def item_steps(t, i, sbi):
            rp, c = ITEMS[i]
            r, kb, var = geom(t, rp)
            m = t % 2
            st = {}

            def s_pe(_):
                order = [(0, 4)] + [(hh, j) for j in range(4) for hh in (1, 0)] + [(1, 4)]
                for (hh, j) in order:
                    krow = kb + 2 * j
                    ks = (krow // 8) % 3
                    koff = (krow % 8) * 64
                    if j < 4:
                        bnk = sbi * 3 + hh
                        out = bank(bnk)[:, j * 128:(j + 1) * 128]
                    else:
                        bnk = sbi * 3 + 2
                        out = bank(bnk)[:, hh * 128:(hh + 1) * 128]
                    S.op("pe", (lambda h, out=out, hh=hh, ks=ks, koff=koff: h.matmul(out, lhsT=kT[64 * hh:64 * hh + 64, ks, c, koff:koff + 128],
                                                                                   rhs=qT[64 * hh:64 * hh + 64, m, c, rp * 128:(rp + 1) * 128], start=True, stop=True)),
                         reads=[("kT", ks, c), ("q", m, c)], writes=bk(bnk))

            def s_act(_):
                for (bnk, c0, n) in ((sbi * 3, 0, 512), (sbi * 3 + 1, 512, 512), (sbi * 3 + 2, 1024, 256)):
                    S.op("act", (lambda h, bnk=bnk, c0=c0, n=n: h.activation(out=PT[:, sbi, c0:c0 + n], in_=bank(bnk)[:, 0:n], func=AF.Exp)),
                         reads=bk(bnk), writes=[("PT", sbi)])
                if var != 0:
                    wk, wt = wload(biasT_d[var][:, c * 1280:(c + 1) * 1280].rearrange("p (k n) -> p k n", k=1), 1, 1280)
                    S.op("act", lambda h: h.activation(out=wt[:, 0, :], in_=wt[:, 0, :], func=AF.Exp), reads=[wk], writes=[wk])
                    st["e"] = ([wk], wt[:, 0, :])

            def s_dve(_):
                ek, ev = st.get("e", (["biasI"], biasI[:, c * 1280:(c + 1) * 1280]))
                S.op("dve", lambda h: h.tensor_tensor(out=PT[:, sbi, :], in0=PT[:, sbi, :], in1=ev, op=ALU.mult),
                     reads=[("PT", sbi)] + ek, writes=[("PT", sbi)])

            nd = bank(3 * sbi + 2)[:, 256:512]

            ndk = bk(3 * sbi + 2)

            def pv_pe(_):
                for hh in range(2):
                    hd = 2 * c + hh
                    for which in range(2):
                        for j in range(5):
                            krow = kb + 2 * j
                            ks = (krow // 8) % 3
                            blk = (krow % 8) // 2
                            c0 = (hh * 512 + j * 128) if j < 4 else (1024 + hh * 128)
                            pk_ = ("PT", sbi)
                            if which == 0:
                                S.op("pe", (lambda h, hh=hh, hd=hd, ks=ks, blk=blk, c0=c0, j=j: h.matmul(nd[64 * hh:64 * hh + 64, 0:128], lhsT=Vr[:, ks, blk, hd * 64:(hd + 1) * 64],
                                                                                                       rhs=PT[:, sbi, c0:c0 + 128], start=(j == 0), stop=(j == 4))),
                                     reads=[("V", ks, blk, hd // 4), pk_], writes=ndk)
                            else:
                                S.op("pe", (lambda h, hh=hh, c0=c0, j=j: h.matmul(nd[64 * hh:64 * hh + 64, 128:256], lhsT=ones[:, 0:64],
                                                                                   rhs=PT[:, sbi, c0:c0 + 128], start=(j == 0), stop=(j == 4))),
                                     reads=["ones", pk_], writes=ndk)

            def pv_act(_):
                S.op("act", lambda h: h.activation(out=rden[:, sbi, :], in_=nd[:, 128:256], func=AF.Ln), reads=ndk, writes=[("rden", sbi)])
                S.op("act", lambda h: h.activation(out=rden[:, sbi, :], in_=rden[:, sbi, :], func=AF.Exp, scale=-1.0),
                     reads=[("rden", sbi)], writes=[("rden", sbi)])
                S.op("act", lambda h: h.activation(out=numS[:, sbi, :], in_=nd[:, 0:128], func=AF.Copy), reads=ndk, writes=[("numS", sbi)])

            def pv_dve(_):
                S.op("dve", lambda h: h.tensor_tensor(out=attnT[:, m, c, rp * 128:(rp + 1) * 128], in0=numS[:, sbi, :], in1=rden[:, sbi, :], op=ALU.mult),
                     reads=[("numS", sbi), ("rden", sbi)], writes=[("attn", m, c)])

            mk = lambda f, g, k: ((lambda: f(0)), (lambda: g(0)), (lambda: k(0)))
            return [mk(s_pe, s_act, s_dve)], [mk(pv_pe, pv_act, pv_dve)]

        aqueue = [[] for _ in range(NT + 1)]
        apv = [None] * (NT + 1)

        def attn_pop(t, limit):
            if t < 0 or t >= NT or not qdone[t]:
                return None
            q = aqueue[t]
            if not q:
                if anext[t] < limit:
                    i = anext[t]
                    anext[t] += 1
                    sbi = apar[0] % 2
                    apar[0] += 1
                    ss, pv = item_steps(t, i, sbi)
                    q.extend(ss)
                    if apv[t] is not None:
                        q.extend(apv[t])
                    apv[t] = pv
                elif apv[t] is not None and limit >= 16:
                    q.extend(apv[t])
                    apv[t] = None
            if not q:
                return None
            return q.pop(0)

        def fill_pe(t, n):
            rest = []
            if FILL:
                for _ in range(n):
                    stp = attn_pop(t, 8)
                    if stp is None:
                        break
                    stp[0]()
                    rest.append((stp[1], stp[2]))
            return rest

        def attn_run(t):
            while True:
                stp = attn_pop(t, 16)
                if stp is None:
                    break
                stp[0]()
                stp[1]()
                stp[2]()

        def stage_B(t):
            m = t % 2
            s3 = t % 3
            rhs = lambda k: xnm[:, m, k, :]
            rk = lambda k: [("xm", m, k)]
            if t == 0:
                emit_qproj(0)
            if t + 1 < NT:
                emit_qproj(t + 1)
            for g in range(4):
                w = (2, 4, 8, 16)[g]
                L = T + 16
                X = xp[:, s3, g, :]
                S.op("dve", (lambda h, X=X, L=L: h.tensor_tensor(out=ptA[:, 1:L], in0=X[:, 0:L - 1], in1=X[:, 1:L], op=ALU.add)),
                     reads=xpkeys(s3, g), writes=KPTA)
                cur, curk, oth, othk = ptA, KPTA, ptB, KPTB
                lo, hi = 1, L
                sh = 1
                while sh * 2 < w:
                    nlo, nhi = lo + sh, hi - sh
                    S.op("dve", (lambda h, cur=cur, oth=oth, nlo=nlo, nhi=nhi, sh=sh: h.tensor_tensor(out=oth[:, nlo:nhi], in0=cur[:, nlo - sh:nhi - sh],
                                                                                                       in1=cur[:, nlo + sh:nhi + sh], op=ALU.add)),
                         reads=curk, writes=othk)
                    cur, curk, oth, othk = oth, othk, cur, curk
                    lo, hi = nlo, nhi
                    sh *= 2
                assert lo <= 8 and hi >= 8 + T
                S.op("dve", (lambda h, cur=cur, g=g, w=w, X=X: h.scalar_tensor_tensor(out=pooled[:, g, :], in0=cur[:, 8:8 + T], scalar=1.0 / w, in1=X[:, 8:8 + T],
                                                                                 op0=ALU.mult, op1=ALU.subtract)),
                     reads=curk + xpkeys(s3, g), writes=[("F", 4 + g // 2)])
                for (cond, off, io) in ((t == 0, 0, 0), (t == NT - 1, T - 8, 8)):
                    if cond:
                        S.op("dve", (lambda h, cur=cur, g=g, off=off, io=io: h.tensor_tensor(out=etmp[:], in0=cur[:, 8 + off:16 + off], in1=invc[:, g, io:io + 8], op=ALU.mult)),
                             reads=curk + ["invc"], writes=["etmp"])
                        S.op("dve", (lambda h, g=g, off=off, X=X: h.tensor_tensor(out=pooled[:, g, off:off + 8], in0=etmp[:], in1=X[:, 8 + off:16 + off], op=ALU.subtract)),
                             reads=["etmp"] + xpkeys(s3, g), writes=[("F", 4 + g // 2)])
            pwk, pwt = wload(Wd["pool_w"].rearrange("g c d -> c g d"), 4, 128)
            for g in range(4):
                b = next_obank()
                S.op("pe", (lambda h, b=b, g=g: h.matmul(bank(b), lhsT=pwt[:, g, :], rhs=pooled[:, g, :], start=True, stop=True)),
                     reads=[pwk, ("F", 4 + g // 2)], writes=[*bk(b)])
                S.op("dve", (lambda h, b=b, g=g: h.tensor_scalar(out=yp2[:, g, :], in0=bank(b), scalar1=pscale[:, g:g + 1], scalar2=None, op0=ALU.mult)),
                     reads=[*bk(b), "pscale"], writes=[("F", 6 + g // 2)])

            attn_run(t)


            wba, wbp, win = wview("w_br_attn"), wview("w_br_pool"), wview("w_in")
            for db in range(4):
                ka, ta = wload(wba[:, :, db * 256:(db + 1) * 256], 4, 256)
                kp, tp = wload(wbp[:, :, db * 256:(db + 1) * 256], 4, 256)
                kga, tga = wload(win[:, :, 2048 + db * 256:2048 + (db + 1) * 256], 8, 256)
                kgp, tgp = wload(win[:, :, 3072 + db * 256:3072 + (db + 1) * 256], 8, 256)
                for j in range(2):
                    dc = 2 * db + j
                    s = dc % 2
                    b0 = s * 4
                    for k in range(4):
                        S.op("pe", (lambda h, k=k, j=j, b0=b0, ta=ta: h.matmul(bank(b0), lhsT=ta[:, k, j * 128:(j + 1) * 128], rhs=attnT[:, m, k, :], start=(k == 0), stop=(k == 3))),
                             reads=[ka, ("attn", m, k)], writes=pk(b0))
                    for k in range(4):
                        S.op("pe", (lambda h, k=k, j=j, b0=b0, tp=tp: h.matmul(bank(b0 + 1), lhsT=tp[:, k, j * 128:(j + 1) * 128], rhs=yp2[:, k, :], start=(k == 0), stop=(k == 3))),
                             reads=[kp, ("F", 6 + k // 2)], writes=pk(b0 + 1))
                    for (kk, tt, bo) in ((kga, tga, 2), (kgp, tgp, 3)):
                        for k in range(8):
                            S.op("pe", (lambda h, k=k, j=j, b0=b0, tt=tt, bo=bo: h.matmul(bank(b0 + bo), lhsT=tt[:, k, j * 128:(j + 1) * 128], rhs=xnm[:, m, k, :],
                                                                                       start=(k == 0), stop=(k == 7))),
                                 reads=[kk, ("xm", m, k)], writes=pk(b0 + bo))
                    for q in range(2):
                        S.op("act", (lambda h, q=q, s=s, b0=b0: h.activation(out=sig[:, s, q, :], in_=bank(b0 + 2 + q), func=AF.Sigmoid)),
                             reads=pk(b0 + 2 + q), writes=[("sig", s, q)])
                        S.op("dve", (lambda h, q=q, s=s, b0=b0: h.tensor_tensor(out=sig[:, s, q, :], in0=bank(b0 + q), in1=sig[:, s, q, :], op=ALU.mult)),
                             reads=pk(b0 + q) + [("sig", s, q)], writes=[("sig", s, q)])
                    S.op("dve", (lambda h, s=s, dc=dc: h.tensor_tensor(out=xn[:, dc, :], in0=sig[:, s, 0, :], in1=sig[:, s, 1, :], op=ALU.add)),
                         reads=[("sig", s, 0), ("sig", s, 1)], writes=[kxn(dc)])

            tsm = TrailStats(lambda c: fH[:, c, :], kfH)

            def evac_m(ci, b):
                tsm.before_evac()
                S.op("act", lambda h: h.activation(out=fH[:, ci, :], in_=bank(b), func=AF.Copy), reads=[*bk(b)], writes=kfH(ci))
                tsm.after_evac(ci)

            if debug and t == int(os.environ.get("K_DBG_T", "0")):
                def dump(nm, src, keys):
                    S.op("sp", lambda h: h.dma_start(out=dbg[nm], in_=src), reads=keys, writes=[("dbg", nm)], dma_key=("dbg", nm))
                dump("attn", attnT[:, m], [("attn", m, c) for c in range(4)])
                dump("pooled", pooled, [("F", 4), ("F", 5)])
                dump("yp2", yp2, [("F", 6), ("F", 7)])
                dump("merged", xn, [("F", c) for c in range(4)])
                dump("q", qT[:, m], [("q", m, c) for c in range(4)])
                dump("k", kT[:, 0, :, :], [("kT", 0, c) for c in range(4)])
                dump("v", Vr[:, 0, :, :], [("V", 0, b, p_) for b in range(4) for p_ in range(2)])
            proj_fm("w_out", 0, 8, lambda k: xn[:, k, :], lambda k: [kxn(k)], evac_m)
            tsm.finish()
            boundary(t, 3, lambda c: fH[:, c, :], kfH, 4, lambda c: xn[:, c, :], lambda c: [kxn(c)], ft=t + 1, nf1=FB[0], nf2=FB[1], pre_stats=True)
            ffn_body(t, "ffn2")
            boundary(t, 5, lambda c: fF[:, c, :], lambda c: [kF(c)], 6, lambda c: xn[:, c, :], lambda c: [kxn(c)], ft=t + 1, nf1=FB[2], nf2=FB[3], pre_stats=True)
            S.op("pool", lambda h: h.dma_start(out=pTb[:], in_=pTv[:, :, t * T:(t + 1) * T]), writes=["pTb"], dma_key="pTb")
            wp, wgt = wview("ple_wp"), wview("ple_wg")
            tsp = TrailStats(lambda c: fH[:, c, :], kfH)
            for db in range(4):
                kp_, tp_ = wload(wp[:, :, db * 256:(db + 1) * 256], 2, 256)
                kg_, tg_ = wload(wgt[:, :, db * 256:(db + 1) * 256], 8, 256)
                for j in range(2):
                    dc = 2 * db + j
                    s = dc % 2
                    b0 = s * 2
                    for k in range(2):
                        S.op("pe", (lambda h, k=k, j=j, b0=b0, tp_=tp_: h.matmul(bank(b0), lhsT=tp_[:, k, j * 128:(j + 1) * 128], rhs=pTb[:, k, :], start=(k == 0), stop=(k == 1))),
                             reads=[kp_, "pTb"], writes=[*bk(b0)])
                    for k in range(8):
                        S.op("pe", (lambda h, k=k, j=j, b0=b0, tg_=tg_: h.matmul(bank(b0 + 1), lhsT=tg_[:, k, j * 128:(j + 1) * 128], rhs=xn[:, k, :], start=(k == 0), stop=(k == 7))),
                             reads=[kg_, kxn(k)], writes=[*bk(b0 + 1)])
                    S.op("act", (lambda h, s=s, b0=b0: h.activation(out=sig[:, s, 0, :], in_=bank(b0 + 1), func=AF.Sigmoid)),
                         reads=[*bk(b0 + 1)], writes=[("sig", s, 0)])
                    tsp.before_evac()
                    S.op("dve", (lambda h, s=s, b0=b0, dc=dc: h.tensor_tensor(out=fH[:, dc, :], in0=bank(b0), in1=sig[:, s, 0, :], op=ALU.mult)),
                         reads=[*bk(b0), ("sig", s, 0)], writes=kfH(dc))
                    tsp.after_evac(dc)
            tsp.finish()
            def chase(c):
                store_h(t, [c])
                if NH == 2 and c >= 2:
                    load_x(t + 2, [c - 2])

            boundary(t, 7, lambda c: fH[:, c, :], kfH, ft=t + 1, nf1=FB[4], nf2=FB[5], after=chase, pre_stats=True)
            if NH == 2:
                load_x(t + 2, [6, 7])

        if stop_stage <= 1:
            for t in range(nt_run):
                stage_A(t)
                store_h(t)
        else:
            stage_A(0)
            for t in range(nt_run):
                if t + 1 < NT:
                    stage_A(t + 1)
                stage_B(t)
        S.op("sp", None, reads=[("outd", s, c) for s in range(NH) for c in range(8)] + [("dbg", nm) for nm in dbg])
        S.emit(nc, st)
    return nc


def _bias_table(rpb):
    out = np.full((5, 128, 8, 5, 128), NEG, dtype=np.float32)
    kl, kc = np.divmod(np.arange(128), 64)
    ql, qc = np.divmod(np.arange(128), 64)
    cs = np.clip(qc - 8, 0, 48)
    for var, r in enumerate((4, 0, 2, 60, 62)):
        kb = min(max(r - 4, 0), 54)
        for j in range(5):
            key_row = kb + 2 * j + kl
            q_row = r + ql
            rs = np.clip(q_row - 4, 0, 56)
            vr = (key_row[:, None] >= rs[None, :]) & (key_row[:, None] < rs[None, :] + 8)
            vc = (kc[:, None] >= cs[None, :]) & (kc[:, None] < cs[None, :] + 16)
            dr = np.clip(key_row[:, None] - q_row[None, :] + 7, 0, 14)
            dcc = np.clip(kc[:, None] - qc[None, :] + 15, 0, 30)
            valid = vr & vc
            for h in range(8):
                out[var, :, h, j, :] = np.where(valid, rpb[h][dr, dcc], np.float32(NEG))
    segs = []
    for c in range(4):
        segs.append(np.concatenate([out[:, :, 2 * c, 0:4, :].reshape(5, 128, 512), out[:, :, 2 * c + 1, 0:4, :].reshape(5, 128, 512),
                                    out[:, :, 2 * c, 4, :], out[:, :, 2 * c + 1, 4, :]], axis=2))
    return np.ascontiguousarray(np.concatenate(segs, axis=2))


def _chunked(v):
    return np.ascontiguousarray(v.reshape(-1, 128).T)


def prepare_inputs(inputs):
    g = lambda k: np.asarray(inputs[k], dtype=np.float32)
    shared = {
        "ffn1_wg": g("ffn1_w_gate")[0], "ffn1_wu": g("ffn1_w_up")[0], "ffn1_wd": g("ffn1_w_down")[0],
        "w_in": g("w_in")[0], "pool_w": g("pool_w")[0], "w_br_attn": g("w_br_attn")[0], "w_br_pool": g("w_br_pool")[0],
        "w_out": g("w_out")[0],
        "ffn2_wg": g("ffn2_w_gate")[0], "ffn2_wu": g("ffn2_w_up")[0], "ffn2_wd": g("ffn2_w_down")[0],
        "ple_wp": g("ple_w_proj")[0], "ple_wg": g("ple_w_gate")[0],
    }
    names = ["ffn1_pre_g", "ffn1_post_g", "mix_pre_g", "mix_post_g", "ffn2_pre_g", "ffn2_post_g", "ple_pre_g", "ple_post_g"]
    shared["gains"] = np.ascontiguousarray(np.stack([_chunked(g(n)[0]) for n in names], axis=1))
    shared["pscale"] = _chunked(g("pool_scale")[0])
    shared["biasT"] = _bias_table(g("rpb")[0])
    invc = np.zeros((128, 4, 16), np.float32)
    for gi, w in enumerate((2, 4, 8, 16)):
        half = w // 2
        for i in range(8):
            tl = i
            tr = SEQ - 8 + i
            invc[:, gi, i] = 1.0 / (min(tl + half, SEQ) - max(tl - half, 0))
            invc[:, gi, 8 + i] = 1.0 / (min(tr + half, SEQ) - max(tr - half, 0))
    shared["invc"] = invc
    shared["ident"] = np.eye(128, dtype=np.float32)
    shared = {k: np.ascontiguousarray(v) for k, v in shared.items()}
    x = g("x")
    p = g("p")[0]
    maps = []
    for b in range(x.shape[0]):
        m = dict(shared)
        m["xT"] = np.ascontiguousarray(x[b].T)
        m["pT"] = np.ascontiguousarray(p[b].T)
        maps.append(m)
    return maps


_NC_CACHE = {}


def kernel(**inputs):
    maps = prepare_inputs(inputs)
    if "nc" not in _NC_CACHE:
        _NC_CACHE["nc"] = build_program()
    nc = _NC_CACHE["nc"]
    res = run_bass_kernel_spmd(nc, maps, core_ids=list(range(8)))
    out = np.stack([np.ascontiguousarray(r["outT"].T) for r in res.results], axis=0)
    return out.astype(np.float32)
```
